# Optimizing a Trainium2 kernel written in Bass

```python
import jax, jax.numpy as jnp
from jax import lax
import numpy as np

D_MODEL = 2048
BATCH = 8
SEQ = 2048
DEPTH = 2

N_MIXERS = 2
GLA_HEADS = 4
GLA_DK = D_MODEL // 2
GLA_DV = D_MODEL
GLA_DK_HEAD = GLA_DK // GLA_HEADS
GLA_DV_HEAD = GLA_DV // GLA_HEADS
GLA_RANK = 16
GLA_TAU = 16.0
GLA_CHUNK = 64
FOX_HEADS = 16
FOX_HEAD_DIM = D_MODEL // FOX_HEADS
FOX_Q_BLOCK = 128
D_FF = 5632
CONV_WIDTH = 3
NORM_EPS = 1e-6
MOD_SCALE = 0.1

kernel_name = "hybrid_gla_fox_convffn_adaln"


def rmsnorm(x, gain):
    xf = x.astype(jnp.float32)
    xf = xf * lax.rsqrt(jnp.mean(xf * xf, axis=-1, keepdims=True) + NORM_EPS)
    return xf.astype(x.dtype) * gain


def gla_mixer(h, w_in, w_gate, b_gate, g_norm, w_out):
    bsz, seq, _ = h.shape
    n_chunks = seq // GLA_CHUNK
    proj = h @ w_in
    q, k, v, r, a = jnp.split(
        proj, [GLA_DK, 2 * GLA_DK, 2 * GLA_DK + GLA_DV, 2 * GLA_DK + 2 * GLA_DV], axis=-1)
    log_alpha = jax.nn.log_sigmoid((a @ w_gate + b_gate).astype(jnp.float32)) / GLA_TAU

    def chunks(t):
        return t.astype(jnp.float32).reshape(
            bsz, n_chunks, GLA_CHUNK, GLA_HEADS, -1).transpose(0, 3, 1, 2, 4)

    q = chunks(q) * (GLA_DK_HEAD ** -0.5)
    k = chunks(k)
    v = chunks(v)
    b = jnp.cumsum(chunks(log_alpha), axis=3)

    q_dec = q * jnp.exp(b)
    k_inv = k * jnp.exp(-b)
    causal = jnp.tril(jnp.ones((GLA_CHUNK, GLA_CHUNK), dtype=bool))
    attn = jnp.where(causal, jnp.einsum('bhnck,bhnsk->bhncs', q_dec, k_inv), 0.0)
    o_intra = jnp.einsum('bhncs,bhnsv->bhncv', attn, v)

    b_last = b[:, :, :, -1:, :]
    k_end = k * jnp.exp(b_last - b)
    chunk_decay = jnp.exp(b_last[:, :, :, 0, :])

    def step(state, xs):
        qd, kd, vv, dec = xs
        o = jnp.einsum('bhck,bhkv->bhcv', qd, state)
        state = dec[..., None] * state + jnp.einsum('bhck,bhcv->bhkv', kd, vv)
        return state, o

    state0 = jnp.zeros((bsz, GLA_HEADS, GLA_DK_HEAD, GLA_DV_HEAD), jnp.float32)
    xs = tuple(jnp.moveaxis(t, 2, 0) for t in (q_dec, k_end, v, chunk_decay))
    _, o_inter = lax.scan(step, state0, xs)
    o = o_intra + jnp.moveaxis(o_inter, 0, 2)
    o = o.transpose(0, 2, 3, 1, 4).reshape(bsz, seq, GLA_HEADS, GLA_DV_HEAD)
    o = rmsnorm(o, g_norm.reshape(GLA_HEADS, GLA_DV_HEAD).astype(jnp.float32))
    o = o.reshape(bsz, seq, GLA_DV).astype(h.dtype) * jax.nn.silu(r)
    return o @ w_out


def fox_mixer(h, w_in, b_f, q_gain, k_gain, w_out):
    bsz, seq, _ = h.shape
    proj = h @ w_in
    q, k, v, og, fl = jnp.split(proj, [D_MODEL, 2 * D_MODEL, 3 * D_MODEL, 4 * D_MODEL], axis=-1)
    q = rmsnorm(q.reshape(bsz, seq, FOX_HEADS, FOX_HEAD_DIM), q_gain) * (FOX_HEAD_DIM ** -0.5)
    k = rmsnorm(k.reshape(bsz, seq, FOX_HEADS, FOX_HEAD_DIM), k_gain)
    v = v.reshape(bsz, seq, FOX_HEADS, FOX_HEAD_DIM)
    q, k, v = (t.transpose(0, 2, 1, 3) for t in (q, k, v))
    log_f = jax.nn.log_sigmoid((fl + b_f).astype(jnp.float32))
    cum = jnp.cumsum(log_f, axis=1).transpose(0, 2, 1)

    n_blocks = seq // FOX_Q_BLOCK
    q_blocks = q.reshape(bsz, FOX_HEADS, n_blocks, FOX_Q_BLOCK, FOX_HEAD_DIM).transpose(2, 0, 1, 3, 4)
    cum_blocks = cum.reshape(bsz, FOX_HEADS, n_blocks, FOX_Q_BLOCK).transpose(2, 0, 1, 3)
    key_pos = jnp.arange(seq)

    def attend(args):
        qb, cb, blk = args
        q_pos = blk * FOX_Q_BLOCK + jnp.arange(FOX_Q_BLOCK)
        logits = (jnp.einsum('bhqd,bhkd->bhqk', qb, k).astype(jnp.float32)
                  + cb[..., None] - cum[:, :, None, :])
        logits = jnp.where(key_pos[None, :] <= q_pos[:, None], logits, -jnp.inf)
        p = jax.nn.softmax(logits, axis=-1)
        return jnp.einsum('bhqk,bhkd->bhqd', p.astype(v.dtype), v)

    o = lax.map(attend, (q_blocks, cum_blocks, jnp.arange(n_blocks)))
    o = o.transpose(1, 0, 3, 2, 4).reshape(bsz, seq, D_MODEL)
    o = o * jax.nn.sigmoid(og)
    return o @ w_out


def conv_ffn(h, w_up, conv_w, conv_b, w_down):
    u = h @ w_up
    u = lax.conv_general_dilated(
        u, conv_w[:, None, :].astype(u.dtype), window_strides=(1,),
        padding=[(CONV_WIDTH - 1, 0)], dimension_numbers=('NWC', 'WIO', 'NWC'),
        feature_group_count=2 * D_FF) + conv_b
    gate, val = jnp.split(u, 2, axis=-1)
    return (jax.nn.silu(gate) * val) @ w_down


def setup_inputs(seed: int = 0) -> dict:
    key = jax.random.key(seed)
    ks = jax.random.split(key, 24)
    n_gla = (DEPTH + 1) // 2
    n_fox = DEPTH // 2
    f32 = jnp.float32

    def w(k, shape, fan_in, scale=1.0):
        return (scale * fan_in ** -0.5) * jax.random.normal(k, shape, f32)

    def gain(k, shape):
        return 1.0 + 0.05 * jax.random.normal(k, shape, f32)

    gla_in = 2 * GLA_DK + 2 * GLA_DV + GLA_RANK
    fox_in = 4 * D_MODEL + FOX_HEADS
    return {
        "x": jax.random.normal(ks[0], (BATCH, SEQ, D_MODEL), f32),
        "c": jax.random.normal(ks[1], (BATCH, D_MODEL), f32),
        "w_mod": w(ks[2], (DEPTH, D_MODEL, 6 * D_MODEL), D_MODEL, MOD_SCALE),
        "b_mod": 0.02 * jax.random.normal(ks[3], (DEPTH, 6 * D_MODEL), f32),
        "norm_mix": gain(ks[4], (DEPTH, D_MODEL)),
        "norm_ffn": gain(ks[5], (DEPTH, D_MODEL)),
        "gla_w_in": w(ks[6], (n_gla, D_MODEL, gla_in), D_MODEL),
        "gla_w_gate": w(ks[7], (n_gla, GLA_RANK, GLA_DK), GLA_RANK),
        "gla_b_gate": 0.1 * jax.random.normal(ks[8], (n_gla, GLA_DK), f32),
        "gla_norm": gain(ks[9], (n_gla, GLA_DV)),
        "gla_w_out": w(ks[10], (n_gla, GLA_DV, D_MODEL), GLA_DV),
        "fox_w_in": w(ks[11], (n_fox, D_MODEL, fox_in), D_MODEL),
        "fox_b_f": 3.0 + 0.5 * jax.random.normal(ks[12], (n_fox, FOX_HEADS), f32),
        "fox_q_norm": gain(ks[13], (n_fox, FOX_HEAD_DIM)),
        "fox_k_norm": gain(ks[14], (n_fox, FOX_HEAD_DIM)),
        "fox_w_out": w(ks[15], (n_fox, D_MODEL, D_MODEL), D_MODEL),
        "ffn_w_up": w(ks[16], (DEPTH, D_MODEL, 2 * D_FF), D_MODEL),
        "ffn_conv_w": w(ks[17], (DEPTH, CONV_WIDTH, 2 * D_FF), CONV_WIDTH),
        "ffn_conv_b": 0.02 * jax.random.normal(ks[18], (DEPTH, 2 * D_FF), f32),
        "ffn_w_down": w(ks[19], (DEPTH, D_FF, D_MODEL), D_FF),
        "norm_final": gain(ks[20], (D_MODEL,)),
    }


def reference(x, c, w_mod, b_mod, norm_mix, norm_ffn,
              gla_w_in, gla_w_gate, gla_b_gate, gla_norm, gla_w_out,
              fox_w_in, fox_b_f, fox_q_norm, fox_k_norm, fox_w_out,
              ffn_w_up, ffn_conv_w, ffn_conv_b, ffn_w_down, norm_final):
    cond = jax.nn.silu(c)
    for i in range(DEPTH):
        mod = (cond @ w_mod[i] + b_mod[i])[:, None, :]
        sh_m, sc_m, g_m, sh_f, sc_f, g_f = jnp.split(mod, 6, axis=-1)
        h = rmsnorm(x, norm_mix[i]) * (1.0 + sc_m) + sh_m
        j = i // N_MIXERS
        if i % N_MIXERS == 0:
            y = gla_mixer(h, gla_w_in[j], gla_w_gate[j], gla_b_gate[j], gla_norm[j], gla_w_out[j])
        else:
            y = fox_mixer(h, fox_w_in[j], fox_b_f[j], fox_q_norm[j], fox_k_norm[j], fox_w_out[j])
        x = x + (1.0 + g_m) * y
        h = rmsnorm(x, norm_ffn[i]) * (1.0 + sc_f) + sh_f
        x = x + (1.0 + g_f) * conv_ffn(h, ffn_w_up[i], ffn_conv_w[i], ffn_conv_b[i], ffn_w_down[i])
    return rmsnorm(x, norm_final)
```

```python
import numpy as np
from contextlib import ExitStack
import concourse.bass as bass
import concourse.mybir as mybir
from concourse.bass_utils import run_bass_kernel_spmd

F32 = mybir.dt.float32
BF16 = mybir.dt.bfloat16
ALU = mybir.AluOpType
AF = mybir.ActivationFunctionType

P = 128
D = 2048
DC = 16
S = 2048
GT = 512
DFF = 5632
FC = 44
NQ = 4
FQ = FC // NQ
EPS = 1e-6
GLA_IN = 6160
FOX_IN = 8208
BFOX = 16.0


class Op:
    __slots__ = ("eng", "fn", "reads", "writes", "dma", "deps", "sem", "val", "needs_inc")

    def __init__(self, eng, fn, reads, writes, dma):
        self.eng = eng; self.fn = fn; self.reads = reads; self.writes = writes; self.dma = dma
        self.deps = []; self.sem = None; self.val = 0; self.needs_inc = False


class Prog:
    ENGS = ("pe", "act", "dve", "pool", "sp")

    def __init__(self):
        self.ops = []
        self.last_w = {}
        self.readers = {}
        self.barrier_op = None

    def add(self, eng, fn, r=(), w=(), dma=None):
        op = Op(eng, fn, tuple(r), tuple(w), dma)
        deps = set()
        for k in op.reads:
            lw = self.last_w.get(k)
            if lw is not None:
                deps.add(lw)
        for k in op.writes:
            lw = self.last_w.get(k)
            if lw is not None:
                deps.add(lw)
            for rd in self.readers.get(k, ()):
                deps.add(rd)
        for k in op.reads:
            self.readers.setdefault(k, []).append(op)
        for k in op.writes:
            self.last_w[k] = op
            self.readers[k] = []
        if self.barrier_op is not None:
            deps.add(self.barrier_op)
        deps.discard(op)
        for d in deps:
            if d.dma is None and op.dma is None and d.eng == "pe" and op.eng == "pe":
                continue
            op.deps.append(d)
            d.needs_inc = True
        self.ops.append(op)
        return op

    def barrier(self, fn):
        keys = set(self.last_w.keys()) | set(self.readers.keys())
        op = self.add("pool", fn, r=(), w=tuple(keys))
        self.barrier_op = op
        return op

    def emit(self, nc, es, block, final_waits):
        SEM_CAP = 30000
        eng_sem = {}
        eng_cnt = {}
        dma_sem = {}
        dma_cnt = {}
        nsem = [0]

        def new_sem(name):
            nsem[0] += 1
            return es.enter_context(nc.semaphore(f"{name}_{nsem[0]}"))

        for op in self.ops:
            if op.dma is not None:
                if op.dma not in dma_sem:
                    dma_sem[op.dma] = new_sem("d")
                    dma_cnt[op.dma] = 0
                dma_cnt[op.dma] += 16
                op.sem = dma_sem[op.dma]; op.val = dma_cnt[op.dma]
                op.needs_inc = True
            elif op.needs_inc:
                if op.eng not in eng_sem or eng_cnt[op.eng] >= SEM_CAP:
                    eng_sem[op.eng] = new_sem(op.eng)
                    eng_cnt[op.eng] = 0
                eng_cnt[op.eng] += 1
                op.sem = eng_sem[op.eng]; op.val = eng_cnt[op.eng]
        self.nsem = nsem[0]
        handles = {"pe": nc.tensor, "act": nc.scalar, "dve": nc.vector, "pool": nc.gpsimd, "sp": nc.sync}
        deco = {"pe": block.tensor, "act": block.scalar, "dve": block.vector, "pool": block.gpsimd, "sp": block.sync}
        counts = {}
        for eng in self.ENGS:
            myops = [o for o in self.ops if o.eng == eng]
            counts[eng] = len(myops)

            def body(e, myops=myops, eng=eng):
                waited = {}
                for op in myops:
                    need = {}
                    for d in op.deps:
                        key = id(d.sem)
                        if d.val > waited.get(key, (None, 0))[1] and d.val > need.get(key, (None, 0))[1]:
                            need[key] = (d.sem, d.val)
                    for key, (sem, val) in need.items():
                        e.wait_ge(sem, val)
                        waited[key] = (sem, val)
                    ins = op.fn(e)
                    if op.needs_inc:
                        ins.then_inc(op.sem, 16 if op.dma is not None else 1)
                if eng == "sp":
                    for (sem, val) in final_waits():
                        e.wait_ge(sem, val)
            deco[eng](body)
        return counts


def build_nc(NG=4, stages=("gla", "ffn0", "fox", "ffn1"), debug=False):
    nc = bass.Bass("TRN2", target_bir_lowering=False)
    SE = NG * GT
    NT = SE // P

    def din(name, shape):
        return nc.dram_tensor(name, list(shape), F32, kind="ExternalInput").ap()

    x_d = din("x", [S, D]); c_d = din("c", [1, D])
    w_mod_d = din("w_mod", [2, D, 6 * D]); b_mod_d = din("b_mod", [2, 6 * D])
    norm_mix_d = din("norm_mix", [2, D]); norm_ffn_d = din("norm_ffn", [2, D])
    gla_w_in_d = din("gla_w_in", [D, GLA_IN]); gla_w_gate_d = din("gla_w_gate", [16, 1024])
    gla_b_gate_d = din("gla_b_gate", [1, 1024]); gla_norm_d = din("gla_norm", [1, D])
    gla_w_out_d = din("gla_w_out", [D, D])
    fox_w_in_d = din("fox_w_in", [D, FOX_IN]); fox_b_f_d = din("fox_b_f", [1, 16])
    fox_q_norm_d = din("fox_q_norm", [1, 128]); fox_k_norm_d = din("fox_k_norm", [1, 128])
    fox_w_out_d = din("fox_w_out", [D, D])
    ffn_w_up_d = din("ffn_w_up", [2, D, 2 * DFF]); ffn_conv_w_d = din("ffn_conv_w", [2, 3, 2 * DFF])
    ffn_conv_b_d = din("ffn_conv_b", [2, 2 * DFF]); ffn_w_down_d = din("ffn_w_down", [2, DFF, D])
    norm_final_d = din("norm_final", [1, D])
    out_d = nc.dram_tensor("out", [S, D], F32, kind="ExternalOutput").ap()

    def dscr(name, shape, dt=BF16):
        return nc.dram_tensor(name, list(shape), dt, kind=("ExternalOutput" if debug else "Internal")).ap()

    qd_s = dscr("qd_s", [16, P, S])
    ki_s = dscr("ki_s", [16, P, S])
    ke_s = dscr("ke_s", [S, 1024])
    vv_s = dscr("vv_s", [S, D])
    rs_s = dscr("rs_s", [16, P, S])
    yy_s = dscr("yy_s", [16, P, S])
    fxw_s = dscr("fxw_s", [32, P, 16, 256])
    fxv_s = dscr("fxv_s", [4, 4, P, 4, 512])

    pg = Prog()
    es = ExitStack()
    with es:
        def sb(name, shape, dt):
            return es.enter_context(nc.sbuf_tensor(name, list(shape), dt))

        xT = sb("xT", [P, DC, S], F32)
        hT = sb("hT", [P, DC, GT], BF16)
        wring = sb("wring", [P, 4, 2048], BF16)
        work = sb("work", [P, 18432], BF16)
        ident_f = sb("ident_f", [P, P], F32)
        ident_b = sb("ident_b", [P, P], BF16)
        ones_f = sb("ones_f", [P, P], F32)
        ones_b = sb("ones_b", [P, P], BF16)
        mixc = sb("mixc", [P, 2304], BF16)
        wg_aug = mixc[0:17, 0:1024]
        aT_aug = mixc[0:17, 1024:1536]
        decay = mixc[:, 1536:2048].bitcast(F32).rearrange("p (a b) -> p a b", a=8)
        ucm = mixc[:, 2048:2304].bitcast(F32)
        nlf_all = mixc[:, 0:512].bitcast(F32).rearrange("p (a b) -> p a b", a=16)
        ncum = mixc[:, 512:1024].bitcast(F32).rearrange("p (a b) -> p a b", a=16)
        uf = mixc[:, 1024:1280].bitcast(F32)
        negm = mixc[:, 1280:1408]
        cvec = sb("cvec", [P, 16 * 16], F32)
        modc = sb("modc", [P, 2 * 96], F32)
        condc = sb("condc", [P, 16], F32)
        condb = sb("condb", [P, 16], BF16)
        convp = sb("convp", [P, 4, 88], F32)
        halo = sb("halo", [P, 88, 2], F32)
        wsmall = sb("wsmall", [P, 16, 16], BF16)
        rsum = sb("rsum", [P, 16], F32)
        bfb = sb("bfb", [P, 16], F32)
        gcol = sb("gcol", [P, 4], F32)
        dummy = sb("mk_dummy", [P, 8], F32)

        def bar():
            pg.barrier(lambda e: e.memset(dummy[:], 0.0))

        psum = [es.enter_context(nc.psum_tensor(f"ps{i}", [P, 512], F32)) for i in range(8)]

        CV_NMIX, CV_NFFN, CV_GN, CV_NF = 0, 2, 4, 5
        CV_A = 6
        CV_G = 10

        def cv(idx, dc):
            return cvec[:, idx * 16 + dc: idx * 16 + dc + 1]

        def mc(layer, which, dc):
            o = layer * 96 + which * 16 + dc
            return modc[:, o:o + 1]

        def wv(off_bytes, shape, dt):
            n = int(np.prod(shape[1:]))
            if dt == F32:
                a = work[:, off_bytes // 2: off_bytes // 2 + 2 * n].bitcast(F32)
            else:
                a = work[:, off_bytes // 2: off_bytes // 2 + n]
            if len(shape) == 3:
                a = a.rearrange("p (a b) -> p a b", a=shape[1])
            return a

        def setup():
            A = pg.add
            A("pool", lambda e: e.memset(ident_f[:], 0.0), w=["ident_f"])
            A("pool", lambda e: e.affine_select(out=ident_f[:], in_=ident_f[:], pattern=[[-1, P]], compare_op=ALU.not_equal,
                                                 fill=1.0, base=0, channel_multiplier=1), r=["ident_f"], w=["ident_f"])
            A("pool", lambda e: e.tensor_copy(out=ident_b[:], in_=ident_f[:]), r=["ident_f"], w=["ident_b"])
            A("pool", lambda e: e.memset(ones_f[:], 1.0), w=["ones_f"])
            A("pool", lambda e: e.memset(ones_b[:], 1.0), w=["ones_b"])
            with nc.allow_non_contiguous_dma(reason="tiny per-feature vectors"):
                def colload(idx, src_row):
                    A("sp", lambda e: e.dma_start(out=cvec[:, idx * 16:(idx + 1) * 16],
                                                  in_=src_row.rearrange("o (dc p) -> p (o dc)", p=P)),
                      w=[("cvec", idx)], dma=("cv", idx))
                colload(CV_NMIX, norm_mix_d[0:1, :]); colload(CV_NMIX + 1, norm_mix_d[1:2, :])
                colload(CV_NFFN, norm_ffn_d[0:1, :]); colload(CV_NFFN + 1, norm_ffn_d[1:2, :])
                colload(CV_GN, gla_norm_d[0:1, :]); colload(CV_NF, norm_final_d[0:1, :])
                A("sp", lambda e: e.dma_start(out=condc[:], in_=c_d[0:1, :].rearrange("o (dc p) -> p (o dc)", p=P)),
                  w=["condc"], dma="cv_c")
                A("sp", lambda e: e.dma_start(out=gcol[:, 0:1], in_=fox_q_norm_d[0:1, :].rearrange("o p -> p o")),
                  w=["gcol"], dma="cv_g")
                A("sp", lambda e: e.dma_start(out=gcol[:, 1:2], in_=fox_k_norm_d[0:1, :].rearrange("o p -> p o")),
                  w=["gcol"], dma="cv_g")
                A("sp", lambda e: e.dma_start(out=bfb[:], in_=fox_b_f_d[0:1, :].to_broadcast([P, 16])),
                  w=["bfb"], dma="cv_b")
            A("act", lambda e: e.activation(out=condb[:], in_=condc[:], func=AF.Silu), r=["condc"], w=["condb"])
            A("dve", lambda e: e.tensor_scalar(out=gcol[:, 2:3], in0=gcol[:, 0:1], scalar1=float(128 ** -0.5), scalar2=None,
                                               op0=ALU.mult), r=["gcol"], w=["gcol2"])

        def load_x():
            A = pg.add
            for t in range(NT):
                st = wv((t % 2) * 8192, [P, D], F32)
                A("sp", lambda e, st=st, t=t: e.dma_start(out=st, in_=x_d[t * P:(t + 1) * P, :]),
                  w=[("xst", t % 2)], dma=("xst", t % 2))
                for q in range(4):
                    bank = psum[(t * 4 + q) % 8]
                    for j in range(4):
                        dc = q * 4 + j
                        A("pe", lambda e, bank=bank, st=st, dc=dc, j=j: e.transpose(bank[:, j * P:(j + 1) * P], st[:, dc * P:(dc + 1) * P], ident_f[:]),
                          r=[("xst", t % 2), "ident_f"], w=[("ps", (t * 4 + q) % 8)])
                    eng = "act" if q % 2 == 0 else "dve"
                    dst = xT[:, q * 4:(q + 1) * 4, t * P:(t + 1) * P]
                    src = bank[:, :].rearrange("p (a b) -> p a b", a=4)
                    if eng == "act":
                        A("act", lambda e, dst=dst, src=src: e.activation(out=dst, in_=src, func=AF.Copy),
                          r=[("ps", (t * 4 + q) % 8)], w=[("xT", t // 4)])
                    else:
                        A("dve", lambda e, dst=dst, src=src: e.tensor_copy(out=dst, in_=src),
                          r=[("ps", (t * 4 + q) % 8)], w=[("xT", t // 4)])

        def mod_layer(layer):
            A = pg.add
            cnt = 0
            bankc = psum[2]
            for seg in range(6):
                mrow = work[0:1, (seg % 2) * 4096:(seg % 2) * 4096 + 4096].bitcast(F32)
                brow = work[0:1, 8192 + (seg % 2) * 4096: 8192 + (seg % 2) * 4096 + 4096].bitcast(F32)
                A("sp", lambda e, brow=brow, seg=seg: e.dma_start(out=brow, in_=b_mod_d[layer:layer + 1, seg * D:(seg + 1) * D]),
                  w=[("brow", seg % 2)], dma=("brow", seg % 2))
                for nq in range(4):
                    nb = seg * 4 + nq
                    bank = psum[nb % 2]
                    for kq in range(4):
                        slot = cnt % 4; cnt += 1
                        wsl = wring[:, slot, :].rearrange("p (a b) -> p a b", a=4)
                        src = w_mod_d[layer, kq * 512:(kq + 1) * 512, nb * 512:(nb + 1) * 512].rearrange("(a p) n -> p a n", p=P)
                        A("pool", lambda e, wsl=wsl, src=src: e.dma_start(out=wsl, in_=src), w=[("wr", slot)], dma=("wr", slot))
                        for a in range(4):
                            kc = kq * 4 + a
                            A("pe", lambda e, bank=bank, kc=kc, wsl=wsl, a=a: e.matmul(bank[0:1, :], lhsT=condb[:, kc:kc + 1], rhs=wsl[:, a, :],
                                                                                      start=(kc == 0), stop=(kc == 15)),
                              r=["condb", ("wr", slot)], w=[("ps", nb % 2)])
                    A("dve", lambda e, bank=bank, nq=nq, mrow=mrow, brow=brow: e.tensor_tensor(out=mrow[0:1, nq * 512:(nq + 1) * 512], in0=bank[0:1, :],
                                                                                          in1=brow[0:1, nq * 512:(nq + 1) * 512], op=ALU.add),
                      r=[("ps", nb % 2), ("brow", seg % 2)], w=[("mrow", seg % 2)])
                for j in range(16):
                    A("pe", lambda e, j=j, seg=seg, mrow=mrow: e.matmul(bankc[:, seg * 16 + j: seg * 16 + j + 1], lhsT=mrow[0:1, j * P:(j + 1) * P], rhs=ones_f[0:1, 0:1],
                                                                      start=True, stop=True), r=[("mrow", seg % 2), "ones_f"], w=[("ps", 2)])
            A("dve", lambda e: e.tensor_copy(out=modc[:, layer * 96:(layer + 1) * 96], in_=bankc[:, 0:96]), r=[("ps", 2)], w=[("modc", layer)])
            mo = layer * 96
            for (which, cvn, dst) in ((1, CV_NMIX + layer, CV_A + layer), (4, CV_NFFN + layer, CV_A + 2 + layer)):
                A("dve", lambda e, which=which, cvn=cvn, dst=dst: e.scalar_tensor_tensor(
                    out=cvec[:, dst * 16:(dst + 1) * 16], in0=modc[:, mo + which * 16: mo + (which + 1) * 16], scalar=1.0,
                    in1=cvec[:, cvn * 16:(cvn + 1) * 16], op0=ALU.add, op1=ALU.mult),
                  r=[("modc", layer), ("cvec", cvn)], w=[("cvec", dst)])
            for (which, dst) in ((2, CV_G + layer), (5, CV_G + 2 + layer)):
                A("dve", lambda e, which=which, dst=dst: e.tensor_scalar(
                    out=cvec[:, dst * 16:(dst + 1) * 16], in0=modc[:, mo + which * 16: mo + (which + 1) * 16], scalar1=1.0, scalar2=None,
                    op0=ALU.add), r=[("modc", layer)], w=[("cvec", dst)])

        W_SQ = 0
        W_RSTD = 2048
        W_TMP = 4096
        W_FREE = 8192

        def rstd_group(g, ncols, c0):
            A = pg.add
            bank = psum[7]
            for dc in range(DC):
                sq = wv(W_SQ + (dc % 2) * 1024, [P, 512], BF16)[:, 0:ncols]
                A("act", lambda e, sq=sq, dc=dc: e.activation(out=sq, in_=xT[:, dc, c0:c0 + ncols], func=AF.Square),
                  r=[("xT", c0 // GT)], w=[("sq", dc % 2)])
                A("pe", lambda e, sq=sq, dc=dc: e.matmul(bank[:, 0:ncols], lhsT=ones_b[:], rhs=sq, start=(dc == 0), stop=(dc == 15)),
                  r=[("sq", dc % 2), "ones_b"], w=[("ps", 7)])
            rstd = wv(W_RSTD, [P, 512], F32)[:, 0:ncols]
            A("act", lambda e: e.activation(out=rstd, in_=bank[:, 0:ncols], func=AF.Ln, bias=float(EPS), scale=1.0 / D), r=[("ps", 7)], w=["rstd"])
            A("act", lambda e: e.activation(out=rstd, in_=rstd, func=AF.Exp, scale=-0.5), r=["rstd"], w=["rstd"])
            return rstd

        def norm_mod(g, a_idx, sh_layer, sh_which):
            A = pg.add
            rstd = rstd_group(g, GT, g * GT)
            for dc in range(DC):
                tmp = wv(W_TMP + (dc % 2) * 2048, [P, 512], F32)
                A("dve", lambda e, tmp=tmp, dc=dc: e.tensor_tensor(out=tmp, in0=xT[:, dc, g * GT:(g + 1) * GT], in1=rstd, op=ALU.mult),
                  r=[("xT", g), "rstd"], w=[("tmp", dc % 2)])
                A("act", lambda e, tmp=tmp, dc=dc: e.activation(out=hT[:, dc, :], in_=tmp, func=AF.Identity,
                                                                scale=cv(a_idx, dc), bias=mc(sh_layer, sh_which, dc)),
                  r=[("tmp", dc % 2), ("cvec", a_idx), ("modc", sh_layer)], w=["hT"])

        wcnt = [0]

        def load_w_chunk(src_ap):
            slot = wcnt[0] % 4; wcnt[0] += 1
            view = wring[:, slot, :].rearrange("p (a b) -> p a b", a=16)
            pg.add("pool", lambda e: e.dma_start(out=view, in_=src_ap.rearrange("(a p) n -> p a n", p=P)),
                   w=[("wr", slot)], dma=("wr", slot))
            return slot, view

        def load_w_rows(src_ap, nrow_chunks):
            slot = wcnt[0] % 4; wcnt[0] += 1
            view = wring[:, slot, 0:nrow_chunks * P].rearrange("p (a b) -> p a b", a=nrow_chunks)
            pg.add("pool", lambda e: e.dma_start(out=view, in_=src_ap.rearrange("(a p) n -> p a n", p=P)),
                   w=[("wr", slot)], dma=("wr", slot))
            return slot, view

        def load_w_wide(src_ap):
            slot = wcnt[0] % 4; wcnt[0] += 1
            view = wring[:, slot, :].rearrange("p (a b) -> p a b", a=4)
            pg.add("pool", lambda e: e.dma_start(out=view, in_=src_ap.rearrange("(a p) n -> p a n", p=P)),
                   w=[("wr", slot)], dma=("wr", slot))
            return slot, view

        pcnt = [0]
        conv_done = [False]

        def convert_fox_weights():
            if conv_done[0]:
                return []
            conv_done[0] = True
            todo = []
            for pr in list(range(16)) + list(range(24, 32)):
                todo.append(lambda pr=pr: pg.add("pool", lambda e: e.dma_start(out=fxw_s[pr], in_=fox_w_in_d[:, pr * 256:(pr + 1) * 256].rearrange("(a p) n -> p a n", p=P)),
                                                 w=[("fxw", pr)], dma=("cvt", pr % 4)))
            for cb in range(4):
                for kq in range(4):
                    todo.append(lambda cb=cb, kq=kq: pg.add("pool", lambda e: e.dma_start(
                        out=fxv_s[cb, kq], in_=fox_w_in_d[kq * 512:(kq + 1) * 512, 2 * D + cb * 512: 2 * D + (cb + 1) * 512].rearrange("(a p) n -> p a n", p=P)),
                        w=[("fxv", cb, kq)], dma=("cvt", (cb * 4 + kq) % 4)))
            return todo

        def load_w_pair(pr):
            wcnt[0] = (wcnt[0] + 1) // 2 * 2
            s0 = wcnt[0] % 4; wcnt[0] += 2
            view = wring[:, s0:s0 + 2, :].rearrange("p a b -> p (a b)").rearrange("p (a b) -> p a b", a=16)
            keys = [("wr", s0), ("wr", s0 + 1)]
            pg.add("pool", lambda e: e.dma_start(out=view, in_=fxw_s[pr]), r=[("fxw", pr)], w=keys, dma=("wr", s0))
            return keys, view

        def proj_mm(lhs_fn, wkeys, ncols=P):
            b = 4 + (pcnt[0] % 2); pcnt[0] += 1
            for kc in range(DC):
                pg.add("pe", lambda e, kc=kc: e.matmul(psum[b][0:ncols, :], lhsT=lhs_fn(kc), rhs=hT[:, kc, :], start=(kc == 0), stop=(kc == 15)),
                       r=list(wkeys) + ["hT"], w=[("ps", b)])
            return b, psum[b][0:ncols, :]

        def proj_fm(w_dram, col0, ncols=P):
            b = 4 + (pcnt[0] % 2); pcnt[0] += 1
            slot, view = load_w_chunk(w_dram[:, col0:col0 + P])
            for kc in range(DC):
                pg.add("pe", lambda e, kc=kc: e.matmul(psum[b][0:ncols, :], lhsT=view[:, kc, 0:ncols], rhs=hT[:, kc, :],
                                                      start=(kc == 0), stop=(kc == 15)),
                       r=[("wr", slot), "hT"], w=[("ps", b)])
            return b, psum[b][0:ncols, :]

        stcnt = {}

        def stage_store(name, nslots, off, shape, dt, fill, dram_ap, dram_keys, extra_r=()):
            i = stcnt.get(name, 0); stcnt[name] = i + 1
            sl = i % nslots
            nbytes = int(np.prod(shape[1:])) * (4 if dt == F32 else 2)
            st = wv(off + sl * nbytes, shape, dt)
            key = (name, sl)
            fill(st, key)
            pg.add("sp", lambda e: e.dma_start(out=dram_ap, in_=st), r=[key], w=list(dram_keys), dma=key)

        def proj_tm(w_dram, col0, g, dst_dram, dst_col0, off, name, nslots, pre=None):
            A = pg.add
            for kq in range(4):
                if pre is None:
                    slot, view = load_w_wide(w_dram[kq * 512:(kq + 1) * 512, col0:col0 + 512])
                else:
                    slot = wcnt[0] % 4; wcnt[0] += 1
                    view = wring[:, slot, :].rearrange("p (a b) -> p a b", a=4)
                    pg.add("pool", lambda e, view=view, kq=kq: e.dma_start(out=view, in_=fxv_s[pre, kq]), r=[("fxv", pre, kq)], w=[("wr", slot)], dma=("wr", slot))
                for tl in range(4):
                    for a in range(4):
                        kc = kq * 4 + a
                        A("pe", lambda e, tl=tl, kc=kc, a=a, view=view: e.matmul(psum[tl][:, :], lhsT=hT[:, kc, tl * P:(tl + 1) * P], rhs=view[:, a, :],
                                                                                start=(kc == 0), stop=(kc == 15)),
                          r=[("wr", slot), "hT"], w=[("ps", tl)])
            for tl in range(4):
                t = g * 4 + tl

                def fill(st, key, tl=tl):
                    eng = "act" if tl % 2 == 0 else "dve"
                    if eng == "act":
                        A("act", lambda e: e.activation(out=st, in_=psum[tl][:, :], func=AF.Copy), r=[("ps", tl)], w=[key])
                    else:
                        A("dve", lambda e: e.tensor_copy(out=st, in_=psum[tl][:, :]), r=[("ps", tl)], w=[key])
                stage_store(name, nslots, off, [P, 512], BF16, fill, dst_dram[t * P:(t + 1) * P, dst_col0:dst_col0 + 512], [("vv", g, dst_col0 // 512, tl)])

        def out_proj(g, w_dram, g_idx, nh, gain_idx=None):
            A = pg.add
            A("sp", lambda e: e.dma_start(out=hT[:, :, :], in_=yy_s[:, :, g * GT:(g + 1) * GT].rearrange("c p t -> p c t")),
              r=[("yy", g, h_) for h_ in range(nh)] + [("yy", g, h_, k_) for h_ in range(nh) for k_ in range(2)], w=["hT"], dma="hTld")
            if gain_idx is not None:
                for dc in range(DC):
                    A("act", lambda e, dc=dc: e.activation(out=hT[:, dc, :], in_=hT[:, dc, :], func=AF.Copy, scale=cv(gain_idx, dc)),
                      r=["hT", ("cvec", gain_idx)], w=["hT"])
            for dc in range(DC):
                b, ps = proj_fm(w_dram, dc * P)
                A("dve", lambda e, ps=ps, dc=dc: e.scalar_tensor_tensor(out=xT[:, dc, g * GT:(g + 1) * GT], in0=ps, scalar=cv(g_idx, dc),
                                                                        in1=xT[:, dc, g * GT:(g + 1) * GT], op0=ALU.mult, op1=ALU.add),
                  r=[("ps", b), ("cvec", g_idx), ("xT", g)], w=[("xT", g)])

        G_BT = W_FREE
        G_EXP = G_BT + 16384
        G_NLA = G_EXP
        G_ST = G_EXP + 4096
        G_KE = G_ST + 4096
        G_KET = G_KE + 2048

        def gla_layer():
            A = pg.add
            layer = 0
            with nc.allow_non_contiguous_dma(reason="tiny gate weights"):
                A("pool", lambda e: e.dma_start(out=wsmall[:], in_=gla_w_in_d[:, 6144:6160].rearrange("(a p) n -> p a n", p=P)),
                  w=["wsmall"], dma="wsm")
            A("pool", lambda e: e.dma_start(out=wg_aug[0:16, :], in_=gla_w_gate_d[:, :]), w=["wg_aug"], dma="wga")
            A("pool", lambda e: e.dma_start(out=wg_aug[16:17, :], in_=gla_b_gate_d[0:1, :]), w=["wg_aug"], dma="wga")
            A("pool", lambda e: e.memset(aT_aug, 1.0), w=["aT_aug"])
            A("pool", lambda e: e.memset(ucm, -1.0 / 16.0), w=["ucm"])
            A("pool", lambda e: e.affine_select(out=ucm, in_=ucm, pattern=[[1, P]], compare_op=ALU.is_ge,
                                                 fill=0.0, base=0, channel_multiplier=-1), r=["ucm"], w=["ucm"])
            A("pool", lambda e: e.memset(ucm[0:64, 64:128], 0.0), r=["ucm"], w=["ucm"])
            for g in range(NG):
                norm_mod(g, CV_A + layer, layer, 0)
                b = 4 + (pcnt[0] % 2); pcnt[0] += 1
                for kc in range(DC):
                    A("pe", lambda e, kc=kc, b=b: e.matmul(psum[b][0:16, :], lhsT=wsmall[:, kc, :], rhs=hT[:, kc, :], start=(kc == 0), stop=(kc == 15)),
                      r=["wsmall", "hT"], w=[("ps", b)])
                A("dve", lambda e, b=b: e.tensor_copy(out=aT_aug[0:16, :], in_=psum[b][0:16, :]), r=[("ps", b)], w=["aT_aug"])
                bT = wv(G_BT, [P, 8, 512], F32)
                nla = wv(G_NLA, [P, 1024], F32)
                for tl in range(4):
                    t = g * 4 + tl
                    for hh in range(2):
                        A("pe", lambda e, hh=hh, tl=tl: e.matmul(psum[hh][:, :], lhsT=aT_aug[:, tl * P:(tl + 1) * P], rhs=wg_aug[:, hh * 512:(hh + 1) * 512],
                                                                start=True, stop=True), r=["aT_aug", "wg_aug"], w=[("ps", hh)])
                        A("act", lambda e, hh=hh: e.activation(out=nla[:, hh * 512:(hh + 1) * 512], in_=psum[hh][:, :], func=AF.Exp, scale=-1.0),
                          r=[("ps", hh)], w=[("ex", hh)])
                        A("act", lambda e, hh=hh: e.activation(out=nla[:, hh * 512:(hh + 1) * 512], in_=nla[:, hh * 512:(hh + 1) * 512], func=AF.Ln, bias=1.0),
                          r=[("ex", hh)], w=[("ex", hh)])
                    for kfc in range(8):
                        bb = 2 + kfc // 4
                        A("pe", lambda e, kfc=kfc, bb=bb: e.matmul(psum[bb][:, (kfc % 4) * P:(kfc % 4 + 1) * P], lhsT=nla[:, kfc * P:(kfc + 1) * P], rhs=ucm,
                                                                  start=True, stop=True), r=[("ex", kfc // 4), "ucm"], w=[("ps", bb)])
                    for hb in range(2):
                        src = psum[2 + hb][:, :].rearrange("p (a b) -> p a b", a=4)
                        dst = bT[:, hb * 4:(hb + 1) * 4, tl * P:(tl + 1) * P]
                        A("dve", lambda e, src=src, dst=dst: e.tensor_copy(out=dst, in_=src), r=[("ps", 2 + hb)], w=[("bT", tl)])
                    for cc in range(2):
                        col = tl * P + cc * 64 + 63
                        ch = t * 2 + cc
                        A("act", lambda e, col=col, ch=ch: e.activation(out=decay[:, :, ch:ch + 1], in_=bT[:, :, col:col + 1], func=AF.Exp),
                          r=[("bT", tl)], w=["decay"])
                bkeys = [("bT", i) for i in range(4)]
                for kfc in range(8):
                    b, ps = proj_fm(gla_w_in_d, kfc * P)

                    def fill(st, key, b=b, ps=ps, kfc=kfc):
                        ex = wv(G_EXP + (kfc % 2) * 2048, [P, 512], F32)
                        A("act", lambda e: e.activation(out=ex, in_=bT[:, kfc, :], func=AF.Exp), r=bkeys, w=[("ex", kfc % 2)])
                        A("dve", lambda e: e.scalar_tensor_tensor(out=st, in0=ps, scalar=1.0 / 16.0, in1=ex, op0=ALU.mult, op1=ALU.mult),
                          r=[("ps", b), ("ex", kfc % 2)], w=[key])
                    stage_store("gst", 4, G_ST, [P, 512], BF16, fill, qd_s[kfc, :, g * GT:(g + 1) * GT], [("qd", g, kfc)])
                for kfc in range(8):
                    b, ps = proj_fm(gla_w_in_d, 1024 + kfc * P)
                    kst = {}

                    def fill(st, key, b=b, ps=ps, kfc=kfc):
                        ex = wv(G_EXP + (kfc % 2) * 2048, [P, 512], F32)
                        A("act", lambda e: e.activation(out=ex, in_=bT[:, kfc, :], func=AF.Exp, scale=-1.0), r=bkeys, w=[("ex", kfc % 2)])
                        A("dve", lambda e: e.tensor_tensor(out=st, in0=ps, in1=ex, op=ALU.mult), r=[("ps", b), ("ex", kfc % 2)], w=[key])
                        kst["st"] = st; kst["key"] = key
                    stage_store("gst", 4, G_ST, [P, 512], BF16, fill, ki_s[kfc, :, g * GT:(g + 1) * GT], [("ki", g, kfc)])
                    ke = wv(G_KE + (kfc % 2) * 1024, [P, 512], BF16)
                    dsl = decay[:, kfc, g * 8:(g + 1) * 8]
                    A("dve", lambda e, ke=ke, dsl=dsl, st=kst["st"]: e.tensor_tensor(
                        out=ke.rearrange("p (c j) -> p c j", c=8), in0=st.rearrange("p (c j) -> p c j", c=8),
                        in1=dsl.unsqueeze(2).to_broadcast([P, 8, 64]), op=ALU.mult),
                      r=[kst["key"], "decay"], w=[("ke", kfc % 2)])
                    tb = 6
                    for tl in range(4):
                        A("pe", lambda e, tl=tl, ke=ke: e.transpose(psum[tb][:, :].bitcast(BF16)[:, tl * P:(tl + 1) * P], ke[:, tl * P:(tl + 1) * P], ident_b[:]),
                          r=[("ke", kfc % 2), "ident_b"], w=[("ps", tb)])
                    ket = wv(G_KET + (kfc % 2) * 1024, [P, 4, P], BF16)
                    kkey = ("ket", kfc % 2)
                    A("act", lambda e, ket=ket: e.activation(out=ket, in_=psum[tb][:, :].bitcast(BF16)[:, 0:512].rearrange("p (a b) -> p a b", a=4), func=AF.Copy),
                      r=[("ps", tb)], w=[kkey])
                    A("sp", lambda e, ket=ket, g=g, kfc=kfc: e.dma_start(out=ke_s[g * GT:(g + 1) * GT, kfc * P:(kfc + 1) * P].rearrange("(a p) n -> p a n", p=P), in_=ket),
                      r=[kkey], w=[("kes", g, kfc)], dma=kkey)
                for dvc in range(16):
                    b, ps = proj_fm(gla_w_in_d, 4096 + dvc * P)

                    def fill(st, key, b=b, ps=ps):
                        A("act", lambda e: e.activation(out=st, in_=ps, func=AF.Silu), r=[("ps", b)], w=[key])
                    stage_store("gst", 4, G_ST, [P, 512], BF16, fill, rs_s[dvc, :, g * GT:(g + 1) * GT], [("rs", g, dvc)])
                for cb in range(4):
                    proj_tm(gla_w_in_d, 2048 + cb * 512, g, vv_s, cb * 512, G_ST, "gst", 4)

            bar()
            gla_core(convert_fox_weights())
            bar()
            for g in range(NG):
                out_proj(g, gla_w_out_d, CV_G + layer, 4, gain_idx=CV_GN)

        def gla_core(todo=()):
            A = pg.add
            todo = list(todo)
            state_f = wv(12288, [P, 8, 512], F32)
            state_b = wv(28672, [P, 8, 512], BF16)
            A("pool", lambda e: e.memset(state_f, 0.0), w=[("state_f", h_, k_) for h_ in range(4) for k_ in range(2)])
            A("pool", lambda e: e.memset(state_b, 0.0), w=[("state_b", h_) for h_ in range(4)])
            slots = []
            for si, base in enumerate((hT, wring)):
                flat = base[:, :, :].rearrange("p a b -> p (a b)")
                o = si * 1792
                slots.append(dict(
                    qd=flat[:, 0:1024].rearrange("p (a b) -> p a b", a=2),
                    ki=flat[:, 1024:2048].rearrange("p (a b) -> p a b", a=2),
                    ke=flat[:, 2048:3072].rearrange("p (a b) -> p a b", a=4),
                    vv=flat[:, 3072:5120].rearrange("p (a b) -> p a b", a=4),
                    rs=flat[:, 5120:7168].rearrange("p (a b) -> p a b", a=4),
                    sq=wv(o, [P, 4, P], BF16), rstd=wv(o + 1024, [P, P], F32), at=wv(o + 1536, [P, P], BF16),
                    rr=wv(7680 + si * 2048, [P, 4, P], F32),
                    yst=wv(3584 + si * 2048, [P, 4, 256], BF16),
                    bk0=("gl", si), rk=[("gl", si)] + (["hT"] if si == 0 else [("wr", s_) for s_ in range(4)]),
                    pb=4 * si, si=si))
            giters = [(g_, hp_) for g_ in range(NG) for hp_ in range(2)]

            def emit_tile_loads(k, tl):
                g, hp = giters[k]
                for si in range(2):
                    sl = slots[si]; h = 2 * hp + si
                    tk_ = ("gl", si, tl); bk0 = ("gl", si, tl)
                    c0 = g * GT + tl * P
                    A("sp", lambda e, sl=sl, h=h, c0=c0: e.dma_start(out=sl["qd"][:, :, tl * P:(tl + 1) * P], in_=qd_s[2 * h:2 * h + 2, :, c0:c0 + P].rearrange("c p t -> p c t")),
                      r=[("qd", g, 2 * h), ("qd", g, 2 * h + 1)], w=[tk_], dma=bk0)
                    A("sp", lambda e, sl=sl, h=h, c0=c0: e.dma_start(out=sl["ki"][:, :, tl * P:(tl + 1) * P], in_=ki_s[2 * h:2 * h + 2, :, c0:c0 + P].rearrange("c p t -> p c t")),
                      r=[("ki", g, 2 * h), ("ki", g, 2 * h + 1)], w=[tk_], dma=bk0)
                    A("sp", lambda e, sl=sl, h=h, c0=c0: e.dma_start(out=sl["ke"][:, tl, :], in_=ke_s[c0:c0 + P, h * 256:(h + 1) * 256]),
                      r=[("kes", g, 2 * h), ("kes", g, 2 * h + 1)], w=[tk_], dma=bk0)
                    A("sp", lambda e, sl=sl, h=h, c0=c0: e.dma_start(out=sl["vv"][:, tl, :], in_=vv_s[c0:c0 + P, h * 512:(h + 1) * 512]),
                      r=[("vv", g, h, tl)], w=[tk_], dma=bk0)
                    A("sp", lambda e, sl=sl, h=h, c0=c0: e.dma_start(out=sl["rs"][:, :, tl * P:(tl + 1) * P], in_=rs_s[4 * h:4 * h + 4, :, c0:c0 + P].rearrange("c p t -> p c t")),
                      r=[("rs", g, 4 * h + d_) for d_ in range(4)], w=[tk_], dma=bk0)

            for tl_ in range(4):
                emit_tile_loads(0, tl_)
            it = 0
            for gk in range(len(giters)):
                g, hp = giters[gk]
                if True:
                    it += 1
                    for tl in range(4):
                        t = g * 4 + tl
                        ts_ = slice(tl * P, (tl + 1) * P)

                        def mk(si, tl=tl, t=t, ts_=ts_):
                            sl = slots[si]; h = 2 * hp + si
                            qd, ki, ke, vv, rs = sl["qd"], sl["ki"], sl["ke"], sl["vv"], sl["rs"]
                            rk = [("gl", si, tl)]
                            pbk = sl["pb"]
                            pa = psum[pbk]; po = psum[pbk + 1]
                            ka, ko = ("ps", pbk), ("ps", pbk + 1)
                            at = sl["at"]; atk = ("at", si)

                            def attn():
                                for kfc in range(2):
                                    A("pe", lambda e, kfc=kfc: e.matmul(pa[:, 0:P], lhsT=ki[:, kfc, ts_], rhs=qd[:, kfc, ts_], start=(kfc == 0), stop=(kfc == 1)),
                                      r=rk, w=[ka])
                                first = (si == 0)
                                A("dve", lambda e: e.scalar_tensor_tensor(out=at, in0=pa[:, 0:P], scalar=-16.0, in1=ucm, op0=ALU.mult, op1=ALU.mult), r=[ka, "ucm"],
                                  w=[atk] + ([("tick", it, tl)] if first else []))
                                if first and todo:
                                    A("pool", lambda e: e.memset(dummy[:], 0.0), r=[("tick", it, tl)], w=["mk_dummy"])
                                    todo.pop(0)()

                            def chunk(cc):
                                rows = slice(cc * 64, cc * 64 + 64)
                                ch = t * 2 + cc
                                for dvc in range(4):
                                    oc = slice(dvc * P + cc * 64, dvc * P + cc * 64 + 64)
                                    A("pe", lambda e, dvc=dvc, oc=oc: e.matmul(
                                        po[:, oc], lhsT=vv[rows, tl, dvc * P:(dvc + 1) * P], rhs=at[rows, rows], start=True, stop=False),
                                      r=rk + [atk], w=[ko])
                                    for kfc in range(2):
                                        A("pe", lambda e, kfc=kfc, dvc=dvc, oc=oc: e.matmul(
                                            po[:, oc], lhsT=state_b[:, h * 2 + kfc, dvc * P:(dvc + 1) * P],
                                            rhs=qd[:, kfc, tl * P + cc * 64: tl * P + cc * 64 + 64], start=False, stop=(kfc == 1)),
                                          r=rk + [("state_b", h)], w=[ko])
                                for kfc in range(2):
                                    pb = psum[pbk + 2 + kfc]
                                    kb = ("ps", pbk + 2 + kfc)
                                    A("pe", lambda e, kfc=kfc, pb=pb: e.matmul(
                                        pb[:, :], lhsT=ke[rows, tl, kfc * P:(kfc + 1) * P], rhs=vv[rows, tl, :], start=True, stop=True),
                                      r=rk, w=[kb])
                                    A("dve", lambda e, kfc=kfc, pb=pb: e.scalar_tensor_tensor(
                                        out=state_f[:, h * 2 + kfc, :], in0=state_f[:, h * 2 + kfc, :], scalar=decay[:, h * 2 + kfc, ch:ch + 1],
                                        in1=pb[:, :], op0=ALU.mult, op1=ALU.add),
                                      r=[kb, ("state_f", h, kfc), "decay"], w=[("state_f", h, kfc)])
                                A("act", lambda e: e.activation(out=state_b[:, h * 2:h * 2 + 2, :], in_=state_f[:, h * 2:h * 2 + 2, :], func=AF.Copy),
                                  r=[("state_f", h, 0), ("state_f", h, 1)], w=[("state_b", h)])

                            def norm():
                                sq = sl["sq"]; sqk = ("gsq", si)
                                A("act", lambda e: e.activation(out=sq, in_=po[:, :].rearrange("p (a b) -> p a b", a=4), func=AF.Square), r=[ko], w=[sqk])
                                for dvc in range(4):
                                    A("pe", lambda e, dvc=dvc: e.matmul(pa[:, P:2 * P], lhsT=ones_b[:], rhs=sq[:, dvc, :], start=(dvc == 0), stop=(dvc == 3)),
                                      r=[sqk, "ones_b"], w=[ka])
                                rstd = sl["rstd"]; rk_ = ("grstd", si)
                                A("act", lambda e: e.activation(out=rstd, in_=pa[:, P:2 * P], func=AF.Ln, bias=float(EPS), scale=1.0 / 512.0), r=[ka], w=[rk_])
                                A("act", lambda e: e.activation(out=rstd, in_=rstd, func=AF.Exp, scale=-0.5), r=[rk_], w=[rk_])
                                yst = sl["yst"]; ykey = ("gy", si)
                                yc = slice((tl % 2) * P, (tl % 2 + 1) * P)
                                rr = sl["rr"]; tk = ("gt1", si)
                                A("dve", lambda e: e.tensor_tensor(out=rr, in0=rs[:, :, ts_], in1=rstd.unsqueeze(1).to_broadcast([P, 4, P]), op=ALU.mult),
                                  r=[rk_] + rk, w=[tk])
                                A("dve", lambda e: e.tensor_tensor(out=yst[:, :, yc], in0=po[:, :].rearrange("p (a b) -> p a b", a=4), in1=rr, op=ALU.mult),
                                  r=[ko, tk], w=[ykey])
                                if tl % 2 == 1:
                                    c0 = g * GT + (tl - 1) * P
                                    A("sp", lambda e: e.dma_start(out=yy_s[4 * h:4 * h + 4, :, c0:c0 + 2 * P].rearrange("c p t -> p c t"), in_=yst),
                                      r=[ykey], w=[("yy", g, h, tl // 2)], dma=ykey)
                            return attn, chunk, norm
                        st0 = mk(0); st1 = mk(1)
                        st0[0](); st1[0]()
                        st0[1](0); st1[1](0)
                        st0[1](1); st1[1](1)
                        st0[2](); st1[2]()
                        if gk + 1 < len(giters):
                            emit_tile_loads(gk + 1, tl)
            while todo:
                todo.pop(0)()

        def ffn_layer(layer):
            A = pg.add
            F_ACC = W_FREE
            F_ACT = F_ACC + 8192
            with nc.allow_non_contiguous_dma(reason="conv params"):
                for j in range(4):
                    for qq in range(8):
                        src = (ffn_conv_w_d[layer, j:j + 1, qq * 1408:(qq + 1) * 1408] if j < 3 else ffn_conv_b_d[layer:layer + 1, qq * 1408:(qq + 1) * 1408])
                        A("sp", lambda e, j=j, qq=qq, src=src: e.dma_start(out=convp[:, j, qq * 11:(qq + 1) * 11], in_=src.rearrange("o (c p) -> p (o c)", p=P)),
                          w=["convp"], dma="convp")
            A("pool", lambda e: e.memset(halo[:], 0.0), w=[("halo", fc) for fc in range(88)])
            actT = wv(F_ACT, [P, FQ, 512], BF16)
            for g in range(NG):
                norm_mod(g, CV_A + 2 + layer, layer, 3)
                for q in range(NQ):
                    for j in range(FQ):
                        f = q * FQ + j
                        accs = []
                        for half in range(2):
                            fc = f + half * FC
                            b, ps = proj_fm(ffn_w_up_d[layer], fc * P)
                            acc = wv(F_ACC + ((f % 2) * 2 + half) * 2048, [P, 512], F32)
                            akey = ("acc", f % 2, half)
                            A("act", lambda e, acc=acc, ps=ps, fc=fc: e.activation(out=acc, in_=ps, func=AF.Identity, scale=convp[:, 2, fc:fc + 1], bias=convp[:, 3, fc:fc + 1]),
                              r=[("ps", b), "convp"], w=[akey])
                            A("dve", lambda e, acc=acc, ps=ps, fc=fc: e.scalar_tensor_tensor(out=acc[:, 1:512], in0=ps[:, 0:511], scalar=convp[:, 1, fc:fc + 1], in1=acc[:, 1:512],
                                                                                       op0=ALU.mult, op1=ALU.add), r=[("ps", b), "convp", akey], w=[akey])
                            A("dve", lambda e, acc=acc, ps=ps, fc=fc: e.scalar_tensor_tensor(out=acc[:, 2:512], in0=ps[:, 0:510], scalar=convp[:, 0, fc:fc + 1], in1=acc[:, 2:512],
                                                                                       op0=ALU.mult, op1=ALU.add), r=[("ps", b), "convp", akey], w=[akey])
                            A("dve", lambda e, acc=acc, fc=fc: e.scalar_tensor_tensor(out=acc[:, 0:1], in0=halo[:, fc, 1:2], scalar=convp[:, 1, fc:fc + 1], in1=acc[:, 0:1],
                                                                                op0=ALU.mult, op1=ALU.add), r=[("halo", fc), "convp", akey], w=[akey])
                            A("dve", lambda e, acc=acc, fc=fc: e.scalar_tensor_tensor(out=acc[:, 0:2], in0=halo[:, fc, 0:2], scalar=convp[:, 0, fc:fc + 1], in1=acc[:, 0:2],
                                                                                op0=ALU.mult, op1=ALU.add), r=[("halo", fc), "convp", akey], w=[akey])
                            A("dve", lambda e, ps=ps, fc=fc: e.tensor_copy(out=halo[:, fc, :], in_=ps[:, 510:512]), r=[("ps", b), ("halo", fc)], w=[("halo", fc)])
                            accs.append((acc, akey))
                        (ag, kg), (av, kv) = accs
                        A("act", lambda e, ag=ag: e.activation(out=ag, in_=ag, func=AF.Silu), r=[kg], w=[kg])
                        A("dve", lambda e, ag=ag, av=av, j=j: e.tensor_tensor(out=actT[:, j, :], in0=ag, in1=av, op=ALU.mult), r=[kg, kv], w=["actT"])
                    for dc in range(DC):
                        b = 4 + (pcnt[0] % 2); pcnt[0] += 1
                        slot, view = load_w_rows(ffn_w_down_d[layer, q * FQ * P:(q + 1) * FQ * P, dc * P:(dc + 1) * P], FQ)
                        for j in range(FQ):
                            A("pe", lambda e, j=j, b=b, view=view: e.matmul(psum[b][:, :], lhsT=view[:, j, :], rhs=actT[:, j, :], start=(j == 0), stop=(j == FQ - 1)),
                              r=[("wr", slot), "actT"], w=[("ps", b)])
                        A("dve", lambda e, b=b, dc=dc, g=g: e.scalar_tensor_tensor(out=xT[:, dc, g * GT:(g + 1) * GT], in0=psum[b][:, :], scalar=cv(CV_G + 2 + layer, dc),
                                                                                   in1=xT[:, dc, g * GT:(g + 1) * GT], op0=ALU.mult, op1=ALU.add),
                          r=[("ps", b), ("cvec", CV_G + 2 + layer), ("xT", g)], w=[("xT", g)])

        def fox_layer():
            A = pg.add
            layer = 1
            X_SQ = W_FREE
            X_RS = X_SQ + 2048
            X_ST = X_RS + 4096
            X_Z = 30720
            bias_all = wv(20480, [P, 136, 16], F32)
            with nc.allow_non_contiguous_dma(reason="tiny fl weights"):
                A("pool", lambda e: e.dma_start(out=wsmall[:], in_=fox_w_in_d[:, 8192:8208].rearrange("(a p) n -> p a n", p=P)),
                  w=["wsmall"], dma="wsm")
            for f_ in convert_fox_weights():
                f_()
            A("pool", lambda e: e.memset(rsum[:], 0.0), w=["rsum"])
            A("pool", lambda e: e.memset(uf, 1.0), w=["uf"])
            A("pool", lambda e: e.affine_select(out=uf, in_=uf, pattern=[[1, P]], compare_op=ALU.is_ge,
                                                 fill=0.0, base=0, channel_multiplier=-1), r=["uf"], w=["uf"])
            A("pool", lambda e: e.memset(negm, 0.0), w=["negm"])
            A("pool", lambda e: e.affine_select(out=negm, in_=negm, pattern=[[1, P]], compare_op=ALU.is_ge,
                                                 fill=-30000.0, base=0, channel_multiplier=-1), r=["negm"], w=["negm"])
            for g in range(NG):
                norm_mod(g, CV_A + layer, layer, 0)
                for tl in range(4):
                    t = g * 4 + tl
                    pz = psum[0]
                    for kc in range(DC):
                        A("pe", lambda e, kc=kc, tl=tl: e.matmul(pz[:, 0:16], lhsT=hT[:, kc, tl * P:(tl + 1) * P], rhs=wsmall[:, kc, :], start=(kc == 0), stop=(kc == 15)),
                          r=["hT", "wsmall"], w=[("ps", 0)])
                    zf = wv(X_Z, [P, 16], F32)
                    A("dve", lambda e, zf=zf: e.tensor_tensor(out=zf, in0=pz[:, 0:16], in1=bfb[:], op=ALU.add), r=[("ps", 0), "bfb"], w=["zf"])
                    A("act", lambda e, zf=zf: e.activation(out=zf, in_=zf, func=AF.Exp, scale=-1.0), r=["zf"], w=["zf"])
                    A("act", lambda e, zf=zf, t=t: e.activation(out=nlf_all[:, t, :], in_=zf, func=AF.Ln, bias=1.0), r=["zf"], w=[("nlf", t)])
                    pc = psum[1]
                    A("pe", lambda e: e.matmul(pc[:, 0:16], lhsT=ones_f[:], rhs=rsum[:], start=True, stop=True), r=["ones_f", "rsum"], w=[("ps", 1)])
                    A("pe", lambda e, t=t: e.matmul(pc[:, 16:32], lhsT=ones_f[:], rhs=rsum[:], start=True, stop=False), r=["ones_f", "rsum"], w=[("ps", 1)])
                    A("pe", lambda e, t=t: e.matmul(pc[:, 16:32], lhsT=uf, rhs=nlf_all[:, t, :], start=False, stop=True), r=["uf", ("nlf", t)], w=[("ps", 1)])
                    A("dve", lambda e, t=t: e.tensor_copy(out=ncum[:, t, :], in_=pc[:, 16:32]), r=[("ps", 1)], w=[("ncum", t)])
                    o0 = t * (t + 1) // 2
                    ref = wv(X_Z + 64, [P, 16], F32)
                    A("dve", lambda e, ref=ref: e.tensor_copy(out=ref, in_=pc[:, 0:16]), r=[("ps", 1)], w=["fref"])
                    A("dve", lambda e, t=t, o0=o0, ref=ref: e.scalar_tensor_tensor(
                        out=bias_all[:, o0:o0 + t + 1, :], in0=ncum[:, 0:t + 1, :], scalar=-BFOX,
                        in1=ref.unsqueeze(1).to_broadcast([P, t + 1, 16]), op0=ALU.add, op1=ALU.subtract),
                      r=[("ncum", j) for j in range(t + 1)] + ["fref"], w=[("fbias", t)])
                    A("dve", lambda e, t=t: e.tensor_tensor(out=rsum[:], in0=rsum[:], in1=nlf_all[:, t, :], op=ALU.add), r=["rsum", ("nlf", t)], w=["rsum"])
                pending = []
                for which in range(2):
                    dst = qd_s if which == 0 else ki_s
                    dkey = "qd" if which == 0 else "ki"
                    gsc = gcol[:, 2:3] if which == 0 else gcol[:, 1:2]
                    for h in range(16):
                        if h % 2 == 0:
                            pkeys, pview = load_w_pair(which * 8 + h // 2)
                        b, ps = proj_mm(lambda kc, pview=pview, sub=h % 2: pview[:, kc, sub * P:(sub + 1) * P], pkeys)
                        sq = wv(X_SQ + (h % 2) * 1024, [P, 512], BF16)
                        A("act", lambda e, sq=sq, ps=ps: e.activation(out=sq, in_=ps, func=AF.Square), r=[("ps", b)], w=[("fsq", h % 2)])

                        def post(b=b, ps=ps, sq=sq, h=h, gsc=gsc, dst=dst, dkey=dkey):
                            pm = psum[6 + (h % 2)]
                            A("pe", lambda e: e.matmul(pm[:, :], lhsT=ones_b[:], rhs=sq, start=True, stop=True), r=[("fsq", h % 2), "ones_b"], w=[("ps", 6 + (h % 2))])
                            rs_ = wv(X_RS + (h % 2) * 2048, [P, 512], F32)
                            A("act", lambda e: e.activation(out=rs_, in_=pm[:, :], func=AF.Ln, bias=float(EPS), scale=1.0 / 128.0), r=[("ps", 6 + (h % 2))], w=[("frs", h % 2)])
                            A("act", lambda e: e.activation(out=rs_, in_=rs_, func=AF.Exp, scale=-0.5), r=[("frs", h % 2)], w=[("frs", h % 2)])

                            def fill(st, key):
                                A("dve", lambda e: e.scalar_tensor_tensor(out=st, in0=ps, scalar=gsc, in1=rs_, op0=ALU.mult, op1=ALU.mult),
                                  r=[("ps", b), ("frs", h % 2), "gcol", "gcol2"], w=[key])
                            stage_store("fst", 6, X_ST, [P, 512], BF16, fill, dst[h, :, g * GT:(g + 1) * GT], [(dkey, g, h)])
                        pending.append(post)
                        if len(pending) > 1:
                            pending.pop(0)()
                while pending:
                    pending.pop(0)()
                for h in range(16):
                    if h % 2 == 0:
                        pkeys, pview = load_w_pair(24 + h // 2)
                    b, ps = proj_mm(lambda kc, pview=pview, sub=h % 2: pview[:, kc, sub * P:(sub + 1) * P], pkeys)

                    def fill(st, key, b=b, ps=ps):
                        A("act", lambda e: e.activation(out=st, in_=ps, func=AF.Sigmoid), r=[("ps", b)], w=[key])
                    stage_store("fst", 6, X_ST, [P, 512], BF16, fill, rs_s[h, :, g * GT:(g + 1) * GT], [("rs", g, h)])
                for cb in range(4):
                    proj_tm(fox_w_in_d, 2 * D + cb * 512, g, vv_s, cb * 512, X_ST, "fst", 6, pre=cb)
            if "foxp" in stages:
                return
            bar()
            fox_core(bias_all)
            bar()
            for g in range(NG):
                out_proj(g, fox_w_out_d, CV_G + layer, 16)

        def fox_core(bias_all):
            A = pg.add
            C_PT = W_FREE
            C_RL = C_PT + 1024
            C_T1 = C_RL + 1024
            C_Y = C_T1 + 1024
            bufs = []
            for i, base in enumerate((hT, wring)):
                flat = base[:, :, :].rearrange("p a b -> p (a b)")
                kn = flat[:, 0:2048]
                vv = flat[:, 2048:4096].rearrange("p (a b) -> p a b", a=16)
                qn = flat[:, 4096:4608]
                sg = flat[:, 4608:5120]
                bufs.append((kn, vv, qn, sg))
            ptc = 0
            iters = [(g_, h_) for g_ in range(NG) for h_ in range(16)]

            def emit_loads(k):
                g, h = iters[k]
                nk = (g + 1) * 4
                bi = k % 2
                kn, vv, qn, sg = bufs[bi]
                bk = ("fx", bi)
                wkeys = [bk] + (["hT"] if bi == 0 else [("wr", s_) for s_ in range(4)])
                A("sp", lambda e: e.dma_start(out=kn[:, 0:nk * P], in_=ki_s[h, :, 0:nk * P]),
                  r=[("ki", j, h) for j in range(g + 1)], w=wkeys, dma=bk)
                A("sp", lambda e: e.dma_start(out=vv[:, 0:nk, :], in_=vv_s[0:nk * P, h * P:(h + 1) * P].rearrange("(a p) n -> p a n", p=P)),
                  r=[("vv", j, h // 4, tl_) for j in range(g + 1) for tl_ in range(4)], w=wkeys, dma=bk)
                A("sp", lambda e: e.dma_start(out=qn, in_=qd_s[h, :, g * GT:(g + 1) * GT]), r=[("qd", g, h)], w=wkeys, dma=bk)
                A("sp", lambda e: e.dma_start(out=sg, in_=rs_s[h, :, g * GT:(g + 1) * GT]), r=[("rs", g, h)], w=wkeys, dma=bk)

            emit_loads(0)
            for it_k in range(len(iters)):
                g, h = iters[it_k]
                if it_k + 1 < len(iters):
                    emit_loads(it_k + 1)
                if True:
                    bi = it_k % 2
                    kn, vv, qn, sg = bufs[bi]
                    bk = ("fx", bi)
                    rk = [bk] + (["hT"] if bi == 0 else [("wr", s_) for s_ in range(4)])
                    yi = stcnt.get("fy", 0); stcnt["fy"] = yi + 1
                    yst = wv(C_Y + (yi % 2) * 1024, [P, 512], BF16)
                    ykey = ("fy", yi % 2)
                    pend = []
                    for il in range(4):
                        i = g * 4 + il
                        ob = i % 2
                        po = psum[4 + 2 * ob][:, 0:P]
                        pl = psum[5 + 2 * ob][:, 0:P]
                        okey = ("ps", 4 + 2 * ob); lkey = ("ps", 5 + 2 * ob)

                        def pv(j, pt, ptk, i=i, il=il, ob=ob, po=po, pl=pl, okey=okey, lkey=lkey, vv=vv, sg=sg, yst=yst, ykey=ykey, rk=rk):
                            A("pe", lambda e: e.matmul(po, lhsT=vv[:, j, :], rhs=pt, start=(j == 0), stop=(j == i)), r=rk + [ptk], w=[okey])
                            A("pe", lambda e: e.matmul(pl, lhsT=ones_b[:], rhs=pt, start=(j == 0), stop=(j == i)), r=["ones_b", ptk], w=[lkey])
                            if j == i:
                                rl = wv(C_RL + ob * 512, [P, P], F32)
                                A("dve", lambda e: e.reciprocal(out=rl, in_=pl), r=[lkey], w=[("frl", ob)])
                                t1 = wv(C_T1 + ob * 512, [P, P], F32)
                                A("dve", lambda e: e.tensor_tensor(out=t1, in0=po, in1=rl, op=ALU.mult), r=[okey, ("frl", ob)], w=[("ft1", ob)])
                                A("dve", lambda e: e.tensor_tensor(out=yst[:, il * P:(il + 1) * P], in0=t1, in1=sg[:, il * P:(il + 1) * P], op=ALU.mult),
                                  r=[("ft1", ob)] + rk, w=[ykey])
                        for j in range(i + 1):
                            sl = ptc % 4; ptc += 1
                            pst = psum[sl][:, 0:P]
                            skey = ("ps", sl)
                            A("pe", lambda e, pst=pst, j=j, il=il, kn=kn, qn=qn, i=i: e.matmul(pst, lhsT=kn[:, j * P:(j + 1) * P], rhs=qn[:, il * P:(il + 1) * P],
                                                                                         start=True, stop=(j != i)), r=rk, w=[skey])
                            if j == i:
                                A("pe", lambda e, pst=pst: e.matmul(pst, lhsT=ident_b[:], rhs=negm, start=False, stop=True), r=["ident_b", "negm"], w=[skey])
                            pt = wv(C_PT + (sl % 4) * 256, [P, P], BF16)
                            ptk = ("fpt", sl % 4)
                            o0 = i * (i + 1) // 2 + j
                            A("act", lambda e, pt=pt, pst=pst, o0=o0, h=h: e.activation(out=pt, in_=pst, func=AF.Exp, bias=bias_all[:, o0, h:h + 1]),
                              r=[skey, ("fbias", i)], w=[ptk])
                            pend.append((pv, j, pt, ptk))
                            if len(pend) > 3:
                                f_, j_, pt_, ptk_ = pend.pop(0)
                                f_(j_, pt_, ptk_)
                    while pend:
                        f_, j_, pt_, ptk_ = pend.pop(0)
                        f_(j_, pt_, ptk_)
                    A("sp", lambda e, yst=yst, h=h, g=g: e.dma_start(out=yy_s[h, :, g * GT:(g + 1) * GT], in_=yst), r=[ykey], w=[("yy", g, h)], dma=ykey)

        def final_out():
            A = pg.add
            O_T = W_FREE
            O_ST = O_T + 1024
            for t in range(NT):
                rstd = rstd_group(t // 4, P, t * P)
                ost = wv(O_ST + (t % 2) * 8192, [P, D], F32)
                okey = ("ost", t % 2)
                for q in range(4):
                    bank = psum[q % 4]
                    for j in range(4):
                        dc = q * 4 + j
                        tmp = wv(O_T + (dc % 2) * 512, [P, P], F32)
                        A("dve", lambda e, tmp=tmp, dc=dc, t=t, rstd=rstd: e.scalar_tensor_tensor(out=tmp, in0=xT[:, dc, t * P:(t + 1) * P], scalar=cv(CV_NF, dc), in1=rstd,
                                                                                                op0=ALU.mult, op1=ALU.mult),
                          r=[("xT", t // 4), "rstd", ("cvec", CV_NF)], w=[("otmp", dc % 2)])
                        A("pe", lambda e, tmp=tmp, bank=bank, j=j: e.transpose(bank[:, j * P:(j + 1) * P], tmp, ident_f[:]), r=[("otmp", dc % 2), "ident_f"], w=[("ps", q % 4)])
                    if q % 2 == 0:
                        A("act", lambda e, ost=ost, bank=bank, q=q: e.activation(out=ost[:, q * 512:(q + 1) * 512], in_=bank[:, :], func=AF.Copy), r=[("ps", q % 4)], w=[okey])
                    else:
                        A("dve", lambda e, ost=ost, bank=bank, q=q: e.tensor_copy(out=ost[:, q * 512:(q + 1) * 512], in_=bank[:, :]), r=[("ps", q % 4)], w=[okey])
                A("sp", lambda e, ost=ost, t=t: e.dma_start(out=out_d[t * P:(t + 1) * P, :], in_=ost), r=[okey], w=[("out", t)], dma=okey)

        setup()
        load_x()
        bar()
        if "gla" in stages:
            mod_layer(0)
            bar()
            gla_layer()
            bar()
        if "ffn0" in stages:
            if "gla" not in stages:
                mod_layer(0)
                bar()
            ffn_layer(0)
            bar()
        if "fox" in stages or "foxp" in stages:
            mod_layer(1)
            bar()
            fox_layer()
            bar()
        if "ffn1" in stages:
            if "fox" not in stages:
                mod_layer(1)
                bar()
            ffn_layer(1)
            bar()
        final_out()

        def final_waits():
            res = []
            for k in (("ost", 0), ("ost", 1)):
                ops = [o for o in pg.ops if o.dma == k]
                if ops:
                    res.append((ops[-1].sem, ops[-1].val))
            return res

        block = es.enter_context(nc.Block())
        with nc.allow_non_contiguous_dma(reason="small strided parameter / per-head slices"):
            counts = pg.emit(nc, es, block, final_waits)
        nc._mk_counts = counts
        nc._mk_nsem = pg.nsem
    return nc


_INPUT_NAMES = ["x", "c", "w_mod", "b_mod", "norm_mix", "norm_ffn", "gla_w_in", "gla_w_gate", "gla_b_gate", "gla_norm",
                "gla_w_out", "fox_w_in", "fox_b_f", "fox_q_norm", "fox_k_norm", "fox_w_out", "ffn_w_up", "ffn_conv_w",
                "ffn_conv_b", "ffn_w_down", "norm_final"]


def make_in_maps(inputs, n):
    f = lambda a: np.ascontiguousarray(np.asarray(a, dtype=np.float32))
    shared = {
        "w_mod": f(inputs["w_mod"]), "b_mod": f(inputs["b_mod"]), "norm_mix": f(inputs["norm_mix"]), "norm_ffn": f(inputs["norm_ffn"]),
        "gla_w_in": f(inputs["gla_w_in"][0]), "gla_w_gate": f(inputs["gla_w_gate"][0]), "gla_b_gate": f(inputs["gla_b_gate"]),
        "gla_norm": f(inputs["gla_norm"]), "gla_w_out": f(inputs["gla_w_out"][0]),
        "fox_w_in": f(inputs["fox_w_in"][0]), "fox_b_f": f(inputs["fox_b_f"]), "fox_q_norm": f(inputs["fox_q_norm"]),
        "fox_k_norm": f(inputs["fox_k_norm"]), "fox_w_out": f(inputs["fox_w_out"][0]),
        "ffn_w_up": f(inputs["ffn_w_up"]), "ffn_conv_w": f(inputs["ffn_conv_w"]), "ffn_conv_b": f(inputs["ffn_conv_b"]),
        "ffn_w_down": f(inputs["ffn_w_down"]), "norm_final": f(inputs["norm_final"]).reshape(1, D),
    }
    x = f(inputs["x"]); c = f(inputs["c"])
    maps = []
    for b in range(n):
        m = dict(shared)
        m["x"] = x[b]
        m["c"] = c[b:b + 1]
        maps.append(m)
    return maps


def kernel(**inputs):
    n = 8
    nc = build_nc()
    in_maps = make_in_maps(inputs, n)
    res = run_bass_kernel_spmd(nc, in_maps, core_ids=list(range(n)))
    out = np.stack([np.asarray(r["out"], dtype=np.float32) for r in res.results], axis=0)
    return out
```

```python
import numpy as np
from contextlib import ExitStack
import concourse.bass as bass
import concourse.mybir as mybir
from concourse.bass_utils import run_bass_kernel_spmd

F32 = mybir.dt.float32
BF16 = mybir.dt.bfloat16
ALU = mybir.AluOpType
AF = mybir.ActivationFunctionType

P = 128
D = 2048
DC = 16
S = 2048
GT = 512
DFF = 5632
FC = 44
NQ = 4
FQ = FC // NQ
EPS = 1e-6
GLA_IN = 6160
FOX_IN = 8208
BFOX = 16.0


class Op:
    __slots__ = ("eng", "fn", "reads", "writes", "dma", "deps", "sem", "val", "needs_inc")

    def __init__(self, eng, fn, reads, writes, dma):
        self.eng = eng; self.fn = fn; self.reads = reads; self.writes = writes; self.dma = dma
        self.deps = []; self.sem = None; self.val = 0; self.needs_inc = False


class Prog:
    ENGS = ("pe", "act", "dve", "pool", "sp")

    def __init__(self):
        self.ops = []
        self.last_w = {}
        self.readers = {}
        self.barrier_op = None

    def add(self, eng, fn, r=(), w=(), dma=None):
        op = Op(eng, fn, tuple(r), tuple(w), dma)
        deps = set()
        for k in op.reads:
            lw = self.last_w.get(k)
            if lw is not None:
                deps.add(lw)
        for k in op.writes:
            lw = self.last_w.get(k)
            if lw is not None:
                deps.add(lw)
            for rd in self.readers.get(k, ()):
                deps.add(rd)
        for k in op.reads:
            self.readers.setdefault(k, []).append(op)
        for k in op.writes:
            self.last_w[k] = op
            self.readers[k] = []
        if self.barrier_op is not None:
            deps.add(self.barrier_op)
        deps.discard(op)
        for d in deps:
            if d.dma is None and op.dma is None and d.eng == "pe" and op.eng == "pe":
                continue
            op.deps.append(d)
            d.needs_inc = True
        self.ops.append(op)
        return op

    def barrier(self, fn):
        keys = set(self.last_w.keys()) | set(self.readers.keys())
        op = self.add("pool", fn, r=(), w=tuple(keys))
        self.barrier_op = op
        return op

    def emit(self, nc, es, block, final_waits):
        SEM_CAP = 30000
        eng_sem = {}
        eng_cnt = {}
        dma_sem = {}
        dma_cnt = {}
        nsem = [0]

        def new_sem(name):
            nsem[0] += 1
            return es.enter_context(nc.semaphore(f"{name}_{nsem[0]}"))

        for op in self.ops:
            if op.dma is not None:
                if op.dma not in dma_sem:
                    dma_sem[op.dma] = new_sem("d")
                    dma_cnt[op.dma] = 0
                dma_cnt[op.dma] += 16
                op.sem = dma_sem[op.dma]; op.val = dma_cnt[op.dma]
                op.needs_inc = True
            elif op.needs_inc:
                if op.eng not in eng_sem or eng_cnt[op.eng] >= SEM_CAP:
                    eng_sem[op.eng] = new_sem(op.eng)
                    eng_cnt[op.eng] = 0
                eng_cnt[op.eng] += 1
                op.sem = eng_sem[op.eng]; op.val = eng_cnt[op.eng]
        self.nsem = nsem[0]
        handles = {"pe": nc.tensor, "act": nc.scalar, "dve": nc.vector, "pool": nc.gpsimd, "sp": nc.sync}
        deco = {"pe": block.tensor, "act": block.scalar, "dve": block.vector, "pool": block.gpsimd, "sp": block.sync}
        counts = {}
        for eng in self.ENGS:
            myops = [o for o in self.ops if o.eng == eng]
            counts[eng] = len(myops)

            def body(e, myops=myops, eng=eng):
                waited = {}
                for op in myops:
                    need = {}
                    for d in op.deps:
                        key = id(d.sem)
                        if d.val > waited.get(key, (None, 0))[1] and d.val > need.get(key, (None, 0))[1]:
                            need[key] = (d.sem, d.val)
                    for key, (sem, val) in need.items():
                        e.wait_ge(sem, val)
                        waited[key] = (sem, val)
                    ins = op.fn(e)
                    if op.needs_inc:
                        ins.then_inc(op.sem, 16 if op.dma is not None else 1)
                if eng == "sp":
                    for (sem, val) in final_waits():
                        e.wait_ge(sem, val)
            deco[eng](body)
        return counts


def build_nc(NG=4, stages=("gla", "ffn0", "fox", "ffn1"), debug=False):
    nc = bass.Bass("TRN2", target_bir_lowering=False)
    SE = NG * GT
    NT = SE // P

    def din(name, shape):
        return nc.dram_tensor(name, list(shape), F32, kind="ExternalInput").ap()

    x_d = din("x", [S, D]); c_d = din("c", [1, D])
    w_mod_d = din("w_mod", [2, D, 6 * D]); b_mod_d = din("b_mod", [2, 6 * D])
    norm_mix_d = din("norm_mix", [2, D]); norm_ffn_d = din("norm_ffn", [2, D])
    gla_w_in_d = din("gla_w_in", [D, GLA_IN]); gla_w_gate_d = din("gla_w_gate", [16, 1024])
    gla_b_gate_d = din("gla_b_gate", [1, 1024]); gla_norm_d = din("gla_norm", [1, D])
    gla_w_out_d = din("gla_w_out", [D, D])
    fox_w_in_d = din("fox_w_in", [D, FOX_IN]); fox_b_f_d = din("fox_b_f", [1, 16])
    fox_q_norm_d = din("fox_q_norm", [1, 128]); fox_k_norm_d = din("fox_k_norm", [1, 128])
    fox_w_out_d = din("fox_w_out", [D, D])
    ffn_w_up_d = din("ffn_w_up", [2, D, 2 * DFF]); ffn_conv_w_d = din("ffn_conv_w", [2, 3, 2 * DFF])
    ffn_conv_b_d = din("ffn_conv_b", [2, 2 * DFF]); ffn_w_down_d = din("ffn_w_down", [2, DFF, D])
    norm_final_d = din("norm_final", [1, D])
    out_d = nc.dram_tensor("out", [S, D], F32, kind="ExternalOutput").ap()

    def dscr(name, shape, dt=BF16):
        return nc.dram_tensor(name, list(shape), dt, kind=("ExternalOutput" if debug else "Internal")).ap()

    qd_s = dscr("qd_s", [16, P, S])
    ki_s = dscr("ki_s", [16, P, S])
    ke_s = dscr("ke_s", [S, 1024])
    vv_s = dscr("vv_s", [S, D])
    rs_s = dscr("rs_s", [16, P, S])
    yy_s = dscr("yy_s", [16, P, S])
    fxw_s = dscr("fxw_s", [32, P, 16, 256])
    fxv_s = dscr("fxv_s", [4, 4, P, 4, 512])

    pg = Prog()
    es = ExitStack()
    with es:
        def sb(name, shape, dt):
            return es.enter_context(nc.sbuf_tensor(name, list(shape), dt))

        xT = sb("xT", [P, DC, S], F32)
        hT = sb("hT", [P, DC, GT], BF16)
        wring = sb("wring", [P, 4, 2048], BF16)
        work = sb("work", [P, 18432], BF16)
        ident_f = sb("ident_f", [P, P], F32)
        ident_b = sb("ident_b", [P, P], BF16)
        ones_f = sb("ones_f", [P, P], F32)
        ones_b = sb("ones_b", [P, P], BF16)
        mixc = sb("mixc", [P, 2304], BF16)
        wg_aug = mixc[0:17, 0:1024]
        aT_aug = mixc[0:17, 1024:1536]
        decay = mixc[:, 1536:2048].bitcast(F32).rearrange("p (a b) -> p a b", a=8)
        ucm = mixc[:, 2048:2304].bitcast(F32)
        nlf_all = mixc[:, 0:512].bitcast(F32).rearrange("p (a b) -> p a b", a=16)
        ncum = mixc[:, 512:1024].bitcast(F32).rearrange("p (a b) -> p a b", a=16)
        uf = mixc[:, 1024:1280].bitcast(F32)
        negm = mixc[:, 1280:1408]
        cvec = sb("cvec", [P, 16 * 16], F32)
        modc = sb("modc", [P, 2 * 96], F32)
        condc = sb("condc", [P, 16], F32)
        condb = sb("condb", [P, 16], BF16)
        convp = sb("convp", [P, 4, 88], F32)
        halo = sb("halo", [P, 88, 2], F32)
        wsmall = sb("wsmall", [P, 16, 16], BF16)
        rsum = sb("rsum", [P, 16], F32)
        bfb = sb("bfb", [P, 16], F32)
        gcol = sb("gcol", [P, 4], F32)
        dummy = sb("mk_dummy", [P, 8], F32)

        def bar():
            pg.barrier(lambda e: e.memset(dummy[:], 0.0))

        psum = [es.enter_context(nc.psum_tensor(f"ps{i}", [P, 512], F32)) for i in range(8)]

        CV_NMIX, CV_NFFN, CV_GN, CV_NF = 0, 2, 4, 5
        CV_A = 6
        CV_G = 10

        def cv(idx, dc):
            return cvec[:, idx * 16 + dc: idx * 16 + dc + 1]

        def mc(layer, which, dc):
            o = layer * 96 + which * 16 + dc
            return modc[:, o:o + 1]

        def wv(off_bytes, shape, dt):
            n = int(np.prod(shape[1:]))
            if dt == F32:
                a = work[:, off_bytes // 2: off_bytes // 2 + 2 * n].bitcast(F32)
            else:
                a = work[:, off_bytes // 2: off_bytes // 2 + n]
            if len(shape) == 3:
                a = a.rearrange("p (a b) -> p a b", a=shape[1])
            return a

        def setup():
            A = pg.add
            A("pool", lambda e: e.memset(ident_f[:], 0.0), w=["ident_f"])
            A("pool", lambda e: e.affine_select(out=ident_f[:], in_=ident_f[:], pattern=[[-1, P]], compare_op=ALU.not_equal,
                                                 fill=1.0, base=0, channel_multiplier=1), r=["ident_f"], w=["ident_f"])
            A("pool", lambda e: e.tensor_copy(out=ident_b[:], in_=ident_f[:]), r=["ident_f"], w=["ident_b"])
            A("pool", lambda e: e.memset(ones_f[:], 1.0), w=["ones_f"])
            A("pool", lambda e: e.memset(ones_b[:], 1.0), w=["ones_b"])
            with nc.allow_non_contiguous_dma(reason="tiny per-feature vectors"):
                def colload(idx, src_row):
                    A("sp", lambda e: e.dma_start(out=cvec[:, idx * 16:(idx + 1) * 16],
                                                  in_=src_row.rearrange("o (dc p) -> p (o dc)", p=P)),
                      w=[("cvec", idx)], dma=("cv", idx))
                colload(CV_NMIX, norm_mix_d[0:1, :]); colload(CV_NMIX + 1, norm_mix_d[1:2, :])
                colload(CV_NFFN, norm_ffn_d[0:1, :]); colload(CV_NFFN + 1, norm_ffn_d[1:2, :])
                colload(CV_GN, gla_norm_d[0:1, :]); colload(CV_NF, norm_final_d[0:1, :])
                A("sp", lambda e: e.dma_start(out=condc[:], in_=c_d[0:1, :].rearrange("o (dc p) -> p (o dc)", p=P)),
                  w=["condc"], dma="cv_c")
                A("sp", lambda e: e.dma_start(out=gcol[:, 0:1], in_=fox_q_norm_d[0:1, :].rearrange("o p -> p o")),
                  w=["gcol"], dma="cv_g")
                A("sp", lambda e: e.dma_start(out=gcol[:, 1:2], in_=fox_k_norm_d[0:1, :].rearrange("o p -> p o")),
                  w=["gcol"], dma="cv_g")
                A("sp", lambda e: e.dma_start(out=bfb[:], in_=fox_b_f_d[0:1, :].to_broadcast([P, 16])),
                  w=["bfb"], dma="cv_b")
            A("act", lambda e: e.activation(out=condb[:], in_=condc[:], func=AF.Silu), r=["condc"], w=["condb"])
            A("dve", lambda e: e.tensor_scalar(out=gcol[:, 2:3], in0=gcol[:, 0:1], scalar1=float(128 ** -0.5), scalar2=None,
                                               op0=ALU.mult), r=["gcol"], w=["gcol2"])

        def load_x():
            A = pg.add
            for t in range(NT):
                st = wv((t % 2) * 8192, [P, D], F32)
                A("sp", lambda e, st=st, t=t: e.dma_start(out=st, in_=x_d[t * P:(t + 1) * P, :]),
                  w=[("xst", t % 2)], dma=("xst", t % 2))
                for q in range(4):
                    bank = psum[(t * 4 + q) % 8]
                    for j in range(4):
                        dc = q * 4 + j
                        A("pe", lambda e, bank=bank, st=st, dc=dc, j=j: e.transpose(bank[:, j * P:(j + 1) * P], st[:, dc * P:(dc + 1) * P], ident_f[:]),
                          r=[("xst", t % 2), "ident_f"], w=[("ps", (t * 4 + q) % 8)])
                    eng = "act" if q % 2 == 0 else "dve"
                    dst = xT[:, q * 4:(q + 1) * 4, t * P:(t + 1) * P]
                    src = bank[:, :].rearrange("p (a b) -> p a b", a=4)
                    if eng == "act":
                        A("act", lambda e, dst=dst, src=src: e.activation(out=dst, in_=src, func=AF.Copy),
                          r=[("ps", (t * 4 + q) % 8)], w=[("xT", t // 4)])
                    else:
                        A("dve", lambda e, dst=dst, src=src: e.tensor_copy(out=dst, in_=src),
                          r=[("ps", (t * 4 + q) % 8)], w=[("xT", t // 4)])

        def mod_layer(layer):
            A = pg.add
            cnt = 0
            bankc = psum[2]
            for seg in range(6):
                mrow = work[0:1, (seg % 2) * 4096:(seg % 2) * 4096 + 4096].bitcast(F32)
                brow = work[0:1, 8192 + (seg % 2) * 4096: 8192 + (seg % 2) * 4096 + 4096].bitcast(F32)
                A("sp", lambda e, brow=brow, seg=seg: e.dma_start(out=brow, in_=b_mod_d[layer:layer + 1, seg * D:(seg + 1) * D]),
                  w=[("brow", seg % 2)], dma=("brow", seg % 2))
                for nq in range(4):
                    nb = seg * 4 + nq
                    bank = psum[nb % 2]
                    for kq in range(4):
                        slot = cnt % 4; cnt += 1
                        wsl = wring[:, slot, :].rearrange("p (a b) -> p a b", a=4)
                        src = w_mod_d[layer, kq * 512:(kq + 1) * 512, nb * 512:(nb + 1) * 512].rearrange("(a p) n -> p a n", p=P)
                        A("pool", lambda e, wsl=wsl, src=src: e.dma_start(out=wsl, in_=src), w=[("wr", slot)], dma=("wr", slot))
                        for a in range(4):
                            kc = kq * 4 + a
                            A("pe", lambda e, bank=bank, kc=kc, wsl=wsl, a=a: e.matmul(bank[0:1, :], lhsT=condb[:, kc:kc + 1], rhs=wsl[:, a, :],
                                                                                      start=(kc == 0), stop=(kc == 15)),
                              r=["condb", ("wr", slot)], w=[("ps", nb % 2)])
                    A("dve", lambda e, bank=bank, nq=nq, mrow=mrow, brow=brow: e.tensor_tensor(out=mrow[0:1, nq * 512:(nq + 1) * 512], in0=bank[0:1, :],
                                                                                          in1=brow[0:1, nq * 512:(nq + 1) * 512], op=ALU.add),
                      r=[("ps", nb % 2), ("brow", seg % 2)], w=[("mrow", seg % 2)])
                for j in range(16):
                    A("pe", lambda e, j=j, seg=seg, mrow=mrow: e.matmul(bankc[:, seg * 16 + j: seg * 16 + j + 1], lhsT=mrow[0:1, j * P:(j + 1) * P], rhs=ones_f[0:1, 0:1],
                                                                      start=True, stop=True), r=[("mrow", seg % 2), "ones_f"], w=[("ps", 2)])
            A("dve", lambda e: e.tensor_copy(out=modc[:, layer * 96:(layer + 1) * 96], in_=bankc[:, 0:96]), r=[("ps", 2)], w=[("modc", layer)])
            mo = layer * 96
            for (which, cvn, dst) in ((1, CV_NMIX + layer, CV_A + layer), (4, CV_NFFN + layer, CV_A + 2 + layer)):
                A("dve", lambda e, which=which, cvn=cvn, dst=dst: e.scalar_tensor_tensor(
                    out=cvec[:, dst * 16:(dst + 1) * 16], in0=modc[:, mo + which * 16: mo + (which + 1) * 16], scalar=1.0,
                    in1=cvec[:, cvn * 16:(cvn + 1) * 16], op0=ALU.add, op1=ALU.mult),
                  r=[("modc", layer), ("cvec", cvn)], w=[("cvec", dst)])
            for (which, dst) in ((2, CV_G + layer), (5, CV_G + 2 + layer)):
                A("dve", lambda e, which=which, dst=dst: e.tensor_scalar(
                    out=cvec[:, dst * 16:(dst + 1) * 16], in0=modc[:, mo + which * 16: mo + (which + 1) * 16], scalar1=1.0, scalar2=None,
                    op0=ALU.add), r=[("modc", layer)], w=[("cvec", dst)])

        W_SQ = 0
        W_RSTD = 2048
        W_TMP = 4096
        W_FREE = 8192

        def rstd_group(g, ncols, c0):
            A = pg.add
            bank = psum[7]
            for dc in range(DC):
                sq = wv(W_SQ + (dc % 2) * 1024, [P, 512], BF16)[:, 0:ncols]
                A("act", lambda e, sq=sq, dc=dc: e.activation(out=sq, in_=xT[:, dc, c0:c0 + ncols], func=AF.Square),
                  r=[("xT", c0 // GT)], w=[("sq", dc % 2)])
                A("pe", lambda e, sq=sq, dc=dc: e.matmul(bank[:, 0:ncols], lhsT=ones_b[:], rhs=sq, start=(dc == 0), stop=(dc == 15)),
                  r=[("sq", dc % 2), "ones_b"], w=[("ps", 7)])
            rstd = wv(W_RSTD, [P, 512], F32)[:, 0:ncols]
            A("act", lambda e: e.activation(out=rstd, in_=bank[:, 0:ncols], func=AF.Ln, bias=float(EPS), scale=1.0 / D), r=[("ps", 7)], w=["rstd"])
            A("act", lambda e: e.activation(out=rstd, in_=rstd, func=AF.Exp, scale=-0.5), r=["rstd"], w=["rstd"])
            return rstd

        def norm_mod(g, a_idx, sh_layer, sh_which):
            A = pg.add
            rstd = rstd_group(g, GT, g * GT)
            for dc in range(DC):
                tmp = wv(W_TMP + (dc % 2) * 2048, [P, 512], F32)
                A("dve", lambda e, tmp=tmp, dc=dc: e.tensor_tensor(out=tmp, in0=xT[:, dc, g * GT:(g + 1) * GT], in1=rstd, op=ALU.mult),
                  r=[("xT", g), "rstd"], w=[("tmp", dc % 2)])
                A("act", lambda e, tmp=tmp, dc=dc: e.activation(out=hT[:, dc, :], in_=tmp, func=AF.Identity,
                                                                scale=cv(a_idx, dc), bias=mc(sh_layer, sh_which, dc)),
                  r=[("tmp", dc % 2), ("cvec", a_idx), ("modc", sh_layer)], w=[("hT", dc)])

        wcnt = [0]

        def load_w_chunk(src_ap):
            slot = wcnt[0] % 4; wcnt[0] += 1
            view = wring[:, slot, :].rearrange("p (a b) -> p a b", a=16)
            pg.add("pool", lambda e: e.dma_start(out=view, in_=src_ap.rearrange("(a p) n -> p a n", p=P)),
                   w=[("wr", slot)], dma=("wr", slot))
            return slot, view

        def load_w_rows(src_ap, nrow_chunks):
            slot = wcnt[0] % 4; wcnt[0] += 1
            view = wring[:, slot, 0:nrow_chunks * P].rearrange("p (a b) -> p a b", a=nrow_chunks)
            pg.add("pool", lambda e: e.dma_start(out=view, in_=src_ap.rearrange("(a p) n -> p a n", p=P)),
                   w=[("wr", slot)], dma=("wr", slot))
            return slot, view

        def load_w_wide(src_ap):
            slot = wcnt[0] % 4; wcnt[0] += 1
            view = wring[:, slot, :].rearrange("p (a b) -> p a b", a=4)
            pg.add("pool", lambda e: e.dma_start(out=view, in_=src_ap.rearrange("(a p) n -> p a n", p=P)),
                   w=[("wr", slot)], dma=("wr", slot))
            return slot, view

        pcnt = [0]
        conv_done = [False]

        def convert_fox_weights():
            if conv_done[0]:
                return []
            conv_done[0] = True
            todo = []
            for pr in list(range(16)) + list(range(24, 32)):
                todo.append(lambda pr=pr: pg.add("pool", lambda e: e.dma_start(out=fxw_s[pr], in_=fox_w_in_d[:, pr * 256:(pr + 1) * 256].rearrange("(a p) n -> p a n", p=P)),
                                                 w=[("fxw", pr)], dma=("cvt", pr % 4)))
            for cb in range(4):
                for kq in range(4):
                    todo.append(lambda cb=cb, kq=kq: pg.add("pool", lambda e: e.dma_start(
                        out=fxv_s[cb, kq], in_=fox_w_in_d[kq * 512:(kq + 1) * 512, 2 * D + cb * 512: 2 * D + (cb + 1) * 512].rearrange("(a p) n -> p a n", p=P)),
                        w=[("fxv", cb, kq)], dma=("cvt", (cb * 4 + kq) % 4)))
            return todo

        def load_w_pair(pr):
            wcnt[0] = (wcnt[0] + 1) // 2 * 2
            s0 = wcnt[0] % 4; wcnt[0] += 2
            view = wring[:, s0:s0 + 2, :].rearrange("p a b -> p (a b)").rearrange("p (a b) -> p a b", a=16)
            keys = [("wr", s0), ("wr", s0 + 1)]
            pg.add("pool", lambda e: e.dma_start(out=view, in_=fxw_s[pr]), r=[("fxw", pr)], w=keys, dma=("wr", s0))
            return keys, view

        def proj_mm(lhs_fn, wkeys, ncols=P):
            b = 4 + (pcnt[0] % 2); pcnt[0] += 1
            for kc in range(DC):
                pg.add("pe", lambda e, kc=kc: e.matmul(psum[b][0:ncols, :], lhsT=lhs_fn(kc), rhs=hT[:, kc, :], start=(kc == 0), stop=(kc == 15)),
                       r=list(wkeys) + [("hT", kc)], w=[("ps", b)])
            return b, psum[b][0:ncols, :]

        def proj_fm(w_dram, col0, ncols=P):
            b = 4 + (pcnt[0] % 2); pcnt[0] += 1
            slot, view = load_w_chunk(w_dram[:, col0:col0 + P])
            for kc in range(DC):
                pg.add("pe", lambda e, kc=kc: e.matmul(psum[b][0:ncols, :], lhsT=view[:, kc, 0:ncols], rhs=hT[:, kc, :],
                                                      start=(kc == 0), stop=(kc == 15)),
                       r=[("wr", slot), ("hT", kc)], w=[("ps", b)])
            return b, psum[b][0:ncols, :]

        stcnt = {}

        def stage_store(name, nslots, off, shape, dt, fill, dram_ap, dram_keys, extra_r=()):
            i = stcnt.get(name, 0); stcnt[name] = i + 1
            sl = i % nslots
            nbytes = int(np.prod(shape[1:])) * (4 if dt == F32 else 2)
            st = wv(off + sl * nbytes, shape, dt)
            key = (name, sl)
            fill(st, key)
            pg.add("sp", lambda e: e.dma_start(out=dram_ap, in_=st), r=[key], w=list(dram_keys), dma=key)

        def proj_tm(w_dram, col0, g, dst_dram, dst_col0, off, name, nslots, pre=None):
            A = pg.add
            for kq in range(4):
                if pre is None:
                    slot, view = load_w_wide(w_dram[kq * 512:(kq + 1) * 512, col0:col0 + 512])
                else:
                    slot = wcnt[0] % 4; wcnt[0] += 1
                    view = wring[:, slot, :].rearrange("p (a b) -> p a b", a=4)
                    pg.add("pool", lambda e, view=view, kq=kq: e.dma_start(out=view, in_=fxv_s[pre, kq]), r=[("fxv", pre, kq)], w=[("wr", slot)], dma=("wr", slot))
                for tl in range(4):
                    for a in range(4):
                        kc = kq * 4 + a
                        A("pe", lambda e, tl=tl, kc=kc, a=a, view=view: e.matmul(psum[tl][:, :], lhsT=hT[:, kc, tl * P:(tl + 1) * P], rhs=view[:, a, :],
                                                                                start=(kc == 0), stop=(kc == 15)),
                          r=[("wr", slot), ("hT", kc)], w=[("ps", tl)])
            for tl in range(4):
                t = g * 4 + tl

                def fill(st, key, tl=tl):
                    eng = "act" if tl % 2 == 0 else "dve"
                    if eng == "act":
                        A("act", lambda e: e.activation(out=st, in_=psum[tl][:, :], func=AF.Copy), r=[("ps", tl)], w=[key])
                    else:
                        A("dve", lambda e: e.tensor_copy(out=st, in_=psum[tl][:, :]), r=[("ps", tl)], w=[key])
                stage_store(name, nslots, off, [P, 512], BF16, fill, dst_dram[t * P:(t + 1) * P, dst_col0:dst_col0 + 512], [("vv", g, dst_col0 // 512, tl)])

        def out_proj(g, w_dram, g_idx, nh, gain_idx=None):
            A = pg.add
            A("sp", lambda e: e.dma_start(out=hT[:, :, :], in_=yy_s[:, :, g * GT:(g + 1) * GT].rearrange("c p t -> p c t")),
              r=[("yy", g, h_) for h_ in range(nh)] + [("yy", g, h_, k_) for h_ in range(nh) for k_ in range(2)], w=[("hT", d_) for d_ in range(DC)], dma="hTld")
            if gain_idx is not None:
                for dc in range(DC):
                    A("act", lambda e, dc=dc: e.activation(out=hT[:, dc, :], in_=hT[:, dc, :], func=AF.Copy, scale=cv(gain_idx, dc)),
                      r=[("hT", dc), ("cvec", gain_idx)], w=[("hT", dc)])
            for dc in range(DC):
                b, ps = proj_fm(w_dram, dc * P)
                A("dve", lambda e, ps=ps, dc=dc: e.scalar_tensor_tensor(out=xT[:, dc, g * GT:(g + 1) * GT], in0=ps, scalar=cv(g_idx, dc),
                                                                        in1=xT[:, dc, g * GT:(g + 1) * GT], op0=ALU.mult, op1=ALU.add),
                  r=[("ps", b), ("cvec", g_idx), ("xT", g)], w=[("xT", g)])

        G_BT = W_FREE
        G_EXP = G_BT + 16384
        G_NLA = G_EXP
        G_ST = G_EXP + 4096
        G_KE = G_ST + 4096
        G_KET = G_KE + 2048

        def gla_layer():
            A = pg.add
            layer = 0
            with nc.allow_non_contiguous_dma(reason="tiny gate weights"):
                A("pool", lambda e: e.dma_start(out=wsmall[:], in_=gla_w_in_d[:, 6144:6160].rearrange("(a p) n -> p a n", p=P)),
                  w=["wsmall"], dma="wsm")
            A("pool", lambda e: e.dma_start(out=wg_aug[0:16, :], in_=gla_w_gate_d[:, :]), w=["wg_aug"], dma="wga")
            A("pool", lambda e: e.dma_start(out=wg_aug[16:17, :], in_=gla_b_gate_d[0:1, :]), w=["wg_aug"], dma="wga")
            A("pool", lambda e: e.memset(aT_aug, 1.0), w=["aT_aug"])
            A("pool", lambda e: e.memset(ucm, -1.0 / 16.0), w=["ucm"])
            A("pool", lambda e: e.affine_select(out=ucm, in_=ucm, pattern=[[1, P]], compare_op=ALU.is_ge,
                                                 fill=0.0, base=0, channel_multiplier=-1), r=["ucm"], w=["ucm"])
            A("pool", lambda e: e.memset(ucm[0:64, 64:128], 0.0), r=["ucm"], w=["ucm"])
            for g in range(NG):
                norm_mod(g, CV_A + layer, layer, 0)
                b = 4 + (pcnt[0] % 2); pcnt[0] += 1
                for kc in range(DC):
                    A("pe", lambda e, kc=kc, b=b: e.matmul(psum[b][0:16, :], lhsT=wsmall[:, kc, :], rhs=hT[:, kc, :], start=(kc == 0), stop=(kc == 15)),
                      r=["wsmall", ("hT", kc)], w=[("ps", b)])
                A("dve", lambda e, b=b: e.tensor_copy(out=aT_aug[0:16, :], in_=psum[b][0:16, :]), r=[("ps", b)], w=["aT_aug"])
                bT = wv(G_BT, [P, 8, 512], F32)
                nla = wv(G_NLA, [P, 1024], F32)
                for tl in range(4):
                    t = g * 4 + tl
                    for hh in range(2):
                        A("pe", lambda e, hh=hh, tl=tl: e.matmul(psum[hh][:, :], lhsT=aT_aug[:, tl * P:(tl + 1) * P], rhs=wg_aug[:, hh * 512:(hh + 1) * 512],
                                                                start=True, stop=True), r=["aT_aug", "wg_aug"], w=[("ps", hh)])
                        A("act", lambda e, hh=hh: e.activation(out=nla[:, hh * 512:(hh + 1) * 512], in_=psum[hh][:, :], func=AF.Exp, scale=-1.0),
                          r=[("ps", hh)], w=[("ex", hh)])
                        A("act", lambda e, hh=hh: e.activation(out=nla[:, hh * 512:(hh + 1) * 512], in_=nla[:, hh * 512:(hh + 1) * 512], func=AF.Ln, bias=1.0),
                          r=[("ex", hh)], w=[("ex", hh)])
                    for kfc in range(8):
                        bb = 2 + kfc // 4
                        A("pe", lambda e, kfc=kfc, bb=bb: e.matmul(psum[bb][:, (kfc % 4) * P:(kfc % 4 + 1) * P], lhsT=nla[:, kfc * P:(kfc + 1) * P], rhs=ucm,
                                                                  start=True, stop=True), r=[("ex", kfc // 4), "ucm"], w=[("ps", bb)])
                    for hb in range(2):
                        src = psum[2 + hb][:, :].rearrange("p (a b) -> p a b", a=4)
                        dst = bT[:, hb * 4:(hb + 1) * 4, tl * P:(tl + 1) * P]
                        A("dve", lambda e, src=src, dst=dst: e.tensor_copy(out=dst, in_=src), r=[("ps", 2 + hb)], w=[("bT", tl)])
                    for cc in range(2):
                        col = tl * P + cc * 64 + 63
                        ch = t * 2 + cc
                        A("act", lambda e, col=col, ch=ch: e.activation(out=decay[:, :, ch:ch + 1], in_=bT[:, :, col:col + 1], func=AF.Exp),
                          r=[("bT", tl)], w=["decay"])
                bkeys = [("bT", i) for i in range(4)]
                for kfc in range(8):
                    b, ps = proj_fm(gla_w_in_d, kfc * P)

                    def fill(st, key, b=b, ps=ps, kfc=kfc):
                        ex = wv(G_EXP + (kfc % 2) * 2048, [P, 512], F32)
                        A("act", lambda e: e.activation(out=ex, in_=bT[:, kfc, :], func=AF.Exp), r=bkeys, w=[("ex", kfc % 2)])
                        A("dve", lambda e: e.scalar_tensor_tensor(out=st, in0=ps, scalar=1.0 / 16.0, in1=ex, op0=ALU.mult, op1=ALU.mult),
                          r=[("ps", b), ("ex", kfc % 2)], w=[key])
                    stage_store("gst", 4, G_ST, [P, 512], BF16, fill, qd_s[kfc, :, g * GT:(g + 1) * GT], [("qd", g, kfc)])
                for kfc in range(8):
                    b, ps = proj_fm(gla_w_in_d, 1024 + kfc * P)
                    kst = {}

                    def fill(st, key, b=b, ps=ps, kfc=kfc):
                        ex = wv(G_EXP + (kfc % 2) * 2048, [P, 512], F32)
                        A("act", lambda e: e.activation(out=ex, in_=bT[:, kfc, :], func=AF.Exp, scale=-1.0), r=bkeys, w=[("ex", kfc % 2)])
                        A("dve", lambda e: e.tensor_tensor(out=st, in0=ps, in1=ex, op=ALU.mult), r=[("ps", b), ("ex", kfc % 2)], w=[key])
                        kst["st"] = st; kst["key"] = key
                    stage_store("gst", 4, G_ST, [P, 512], BF16, fill, ki_s[kfc, :, g * GT:(g + 1) * GT], [("ki", g, kfc)])
                    ke = wv(G_KE + (kfc % 2) * 1024, [P, 512], BF16)
                    dsl = decay[:, kfc, g * 8:(g + 1) * 8]
                    A("dve", lambda e, ke=ke, dsl=dsl, st=kst["st"]: e.tensor_tensor(
                        out=ke.rearrange("p (c j) -> p c j", c=8), in0=st.rearrange("p (c j) -> p c j", c=8),
                        in1=dsl.unsqueeze(2).to_broadcast([P, 8, 64]), op=ALU.mult),
                      r=[kst["key"], "decay"], w=[("ke", kfc % 2)])
                    tb = 6
                    for tl in range(4):
                        A("pe", lambda e, tl=tl, ke=ke: e.transpose(psum[tb][:, :].bitcast(BF16)[:, tl * P:(tl + 1) * P], ke[:, tl * P:(tl + 1) * P], ident_b[:]),
                          r=[("ke", kfc % 2), "ident_b"], w=[("ps", tb)])
                    ket = wv(G_KET + (kfc % 2) * 1024, [P, 4, P], BF16)
                    kkey = ("ket", kfc % 2)
                    A("act", lambda e, ket=ket: e.activation(out=ket, in_=psum[tb][:, :].bitcast(BF16)[:, 0:512].rearrange("p (a b) -> p a b", a=4), func=AF.Copy),
                      r=[("ps", tb)], w=[kkey])
                    A("sp", lambda e, ket=ket, g=g, kfc=kfc: e.dma_start(out=ke_s[g * GT:(g + 1) * GT, kfc * P:(kfc + 1) * P].rearrange("(a p) n -> p a n", p=P), in_=ket),
                      r=[kkey], w=[("kes", g, kfc)], dma=kkey)
                for dvc in range(16):
                    b, ps = proj_fm(gla_w_in_d, 4096 + dvc * P)

                    def fill(st, key, b=b, ps=ps):
                        A("act", lambda e: e.activation(out=st, in_=ps, func=AF.Silu), r=[("ps", b)], w=[key])
                    stage_store("gst", 4, G_ST, [P, 512], BF16, fill, rs_s[dvc, :, g * GT:(g + 1) * GT], [("rs", g, dvc)])
                for cb in range(4):
                    proj_tm(gla_w_in_d, 2048 + cb * 512, g, vv_s, cb * 512, G_ST, "gst", 4)

            bar()
            gla_core(convert_fox_weights())
            bar()
            for g in range(NG):
                out_proj(g, gla_w_out_d, CV_G + layer, 4, gain_idx=CV_GN)

        def gla_core(todo=()):
            A = pg.add
            todo = list(todo)
            state_f = wv(12288, [P, 8, 512], F32)
            state_b = wv(28672, [P, 8, 512], BF16)
            A("pool", lambda e: e.memset(state_f, 0.0), w=[("state_f", h_, k_) for h_ in range(4) for k_ in range(2)])
            A("pool", lambda e: e.memset(state_b, 0.0), w=[("state_b", h_) for h_ in range(4)])
            slots = []
            for si, base in enumerate((hT, wring)):
                flat = base[:, :, :].rearrange("p a b -> p (a b)")
                o = si * 1792
                slots.append(dict(
                    qd=flat[:, 0:1024].rearrange("p (a b) -> p a b", a=2),
                    ki=flat[:, 1024:2048].rearrange("p (a b) -> p a b", a=2),
                    ke=flat[:, 2048:3072].rearrange("p (a b) -> p a b", a=4),
                    vv=flat[:, 3072:5120].rearrange("p (a b) -> p a b", a=4),
                    rs=flat[:, 5120:7168].rearrange("p (a b) -> p a b", a=4),
                    sq=wv(o, [P, 4, P], BF16), rstd=wv(o + 1024, [P, P], F32), at=wv(o + 1536, [P, P], BF16),
                    rr=wv(7680 + si * 2048, [P, 4, P], F32),
                    yst=wv(3584 + si * 2048, [P, 4, 256], BF16),
                    bk0=("gl", si), rk=[("gl", si)],
                    pb=4 * si, si=si))
            giters = [(g_, hp_) for g_ in range(NG) for hp_ in range(2)]

            def emit_tile_loads(k, tl):
                g, hp = giters[k]
                for si in range(2):
                    sl = slots[si]; h = 2 * hp + si
                    tk_ = ("gl", si, tl); bk0 = ("gl", si, tl)
                    c0 = g * GT + tl * P
                    A("sp", lambda e, sl=sl, h=h, c0=c0: e.dma_start(out=sl["qd"][:, :, tl * P:(tl + 1) * P], in_=qd_s[2 * h:2 * h + 2, :, c0:c0 + P].rearrange("c p t -> p c t")),
                      r=[("qd", g, 2 * h), ("qd", g, 2 * h + 1)], w=[tk_], dma=bk0)
                    A("sp", lambda e, sl=sl, h=h, c0=c0: e.dma_start(out=sl["ki"][:, :, tl * P:(tl + 1) * P], in_=ki_s[2 * h:2 * h + 2, :, c0:c0 + P].rearrange("c p t -> p c t")),
                      r=[("ki", g, 2 * h), ("ki", g, 2 * h + 1)], w=[tk_], dma=bk0)
                    A("sp", lambda e, sl=sl, h=h, c0=c0: e.dma_start(out=sl["ke"][:, tl, :], in_=ke_s[c0:c0 + P, h * 256:(h + 1) * 256]),
                      r=[("kes", g, 2 * h), ("kes", g, 2 * h + 1)], w=[tk_], dma=bk0)
                    A("sp", lambda e, sl=sl, h=h, c0=c0: e.dma_start(out=sl["vv"][:, tl, :], in_=vv_s[c0:c0 + P, h * 512:(h + 1) * 512]),
                      r=[("vv", g, h, tl)], w=[tk_], dma=bk0)
                    A("sp", lambda e, sl=sl, h=h, c0=c0: e.dma_start(out=sl["rs"][:, :, tl * P:(tl + 1) * P], in_=rs_s[4 * h:4 * h + 4, :, c0:c0 + P].rearrange("c p t -> p c t")),
                      r=[("rs", g, 4 * h + d_) for d_ in range(4)], w=[tk_], dma=bk0)

            for tl_ in range(4):
                emit_tile_loads(0, tl_)
            it = 0
            for gk in range(len(giters)):
                g, hp = giters[gk]
                if True:
                    it += 1
                    for tl in range(4):
                        t = g * 4 + tl
                        ts_ = slice(tl * P, (tl + 1) * P)

                        def mk(si, tl=tl, t=t, ts_=ts_):
                            sl = slots[si]; h = 2 * hp + si
                            qd, ki, ke, vv, rs = sl["qd"], sl["ki"], sl["ke"], sl["vv"], sl["rs"]
                            rk = [("gl", si, tl)]
                            pbk = sl["pb"]
                            pa = psum[pbk]; po = psum[pbk + 1]
                            ka, ko = ("ps", pbk), ("ps", pbk + 1)
                            at = sl["at"]; atk = ("at", si)

                            def attn():
                                for kfc in range(2):
                                    A("pe", lambda e, kfc=kfc: e.matmul(pa[:, 0:P], lhsT=ki[:, kfc, ts_], rhs=qd[:, kfc, ts_], start=(kfc == 0), stop=(kfc == 1)),
                                      r=rk, w=[ka])
                                first = (tl == 0 and si == 0)
                                A("dve", lambda e: e.scalar_tensor_tensor(out=at, in0=pa[:, 0:P], scalar=-16.0, in1=ucm, op0=ALU.mult, op1=ALU.mult), r=[ka, "ucm"],
                                  w=[atk] + ([("tick", it)] if first else []))
                                if first and todo:
                                    A("pool", lambda e: e.memset(dummy[:], 0.0), r=[("tick", it)], w=["mk_dummy"])
                                    for _ in range(3):
                                        if todo:
                                            todo.pop(0)()

                            def chunk(cc):
                                rows = slice(cc * 64, cc * 64 + 64)
                                ch = t * 2 + cc
                                for dvc in range(4):
                                    oc = slice(dvc * P + cc * 64, dvc * P + cc * 64 + 64)
                                    A("pe", lambda e, dvc=dvc, oc=oc: e.matmul(
                                        po[:, oc], lhsT=vv[rows, tl, dvc * P:(dvc + 1) * P], rhs=at[rows, rows], start=True, stop=False),
                                      r=rk + [atk], w=[ko])
                                    for kfc in range(2):
                                        A("pe", lambda e, kfc=kfc, dvc=dvc, oc=oc: e.matmul(
                                            po[:, oc], lhsT=state_b[:, h * 2 + kfc, dvc * P:(dvc + 1) * P],
                                            rhs=qd[:, kfc, tl * P + cc * 64: tl * P + cc * 64 + 64], start=False, stop=(kfc == 1)),
                                          r=rk + [("state_b", h)], w=[ko])
                                for kfc in range(2):
                                    pb = psum[pbk + 2 + kfc]
                                    kb = ("ps", pbk + 2 + kfc)
                                    A("pe", lambda e, kfc=kfc, pb=pb: e.matmul(
                                        pb[:, :], lhsT=ke[rows, tl, kfc * P:(kfc + 1) * P], rhs=vv[rows, tl, :], start=True, stop=True),
                                      r=rk, w=[kb])
                                    A("dve", lambda e, kfc=kfc, pb=pb: e.scalar_tensor_tensor(
                                        out=state_f[:, h * 2 + kfc, :], in0=state_f[:, h * 2 + kfc, :], scalar=decay[:, h * 2 + kfc, ch:ch + 1],
                                        in1=pb[:, :], op0=ALU.mult, op1=ALU.add),
                                      r=[kb, ("state_f", h, kfc), "decay"], w=[("state_f", h, kfc)])
                                A("act", lambda e: e.activation(out=state_b[:, h * 2:h * 2 + 2, :], in_=state_f[:, h * 2:h * 2 + 2, :], func=AF.Copy),
                                  r=[("state_f", h, 0), ("state_f", h, 1)], w=[("state_b", h)])

                            def norm():
                                sq = sl["sq"]; sqk = ("gsq", si)
                                A("act", lambda e: e.activation(out=sq, in_=po[:, :].rearrange("p (a b) -> p a b", a=4), func=AF.Square), r=[ko], w=[sqk])
                                for dvc in range(4):
                                    A("pe", lambda e, dvc=dvc: e.matmul(pa[:, P:2 * P], lhsT=ones_b[:], rhs=sq[:, dvc, :], start=(dvc == 0), stop=(dvc == 3)),
                                      r=[sqk, "ones_b"], w=[ka])
                                rstd = sl["rstd"]; rk_ = ("grstd", si)
                                A("act", lambda e: e.activation(out=rstd, in_=pa[:, P:2 * P], func=AF.Ln, bias=float(EPS), scale=1.0 / 512.0), r=[ka], w=[rk_])
                                A("act", lambda e: e.activation(out=rstd, in_=rstd, func=AF.Exp, scale=-0.5), r=[rk_], w=[rk_])
                                yst = sl["yst"]; ykey = ("gy", si)
                                yc = slice((tl % 2) * P, (tl % 2 + 1) * P)
                                rr = sl["rr"]; tk = ("gt1", si)
                                A("dve", lambda e: e.tensor_tensor(out=rr, in0=rs[:, :, ts_], in1=rstd.unsqueeze(1).to_broadcast([P, 4, P]), op=ALU.mult),
                                  r=[rk_] + rk, w=[tk])
                                A("dve", lambda e: e.tensor_tensor(out=yst[:, :, yc], in0=po[:, :].rearrange("p (a b) -> p a b", a=4), in1=rr, op=ALU.mult),
                                  r=[ko, tk], w=[ykey])
                                if tl % 2 == 1:
                                    c0 = g * GT + (tl - 1) * P
                                    A("sp", lambda e: e.dma_start(out=yy_s[4 * h:4 * h + 4, :, c0:c0 + 2 * P].rearrange("c p t -> p c t"), in_=yst),
                                      r=[ykey], w=[("yy", g, h, tl // 2)], dma=ykey)
                            return attn, chunk, norm
                        st0 = mk(0); st1 = mk(1)
                        st0[0](); st1[0]()
                        st0[1](0); st1[1](0)
                        st0[1](1); st1[1](1)
                        st0[2](); st1[2]()
                        if gk + 1 < len(giters):
                            emit_tile_loads(gk + 1, tl)
            while todo:
                todo.pop(0)()

        def ffn_layer(layer):
            A = pg.add
            F_ACC = W_FREE
            F_ACT = F_ACC + 8192
            with nc.allow_non_contiguous_dma(reason="conv params"):
                for j in range(4):
                    for qq in range(8):
                        src = (ffn_conv_w_d[layer, j:j + 1, qq * 1408:(qq + 1) * 1408] if j < 3 else ffn_conv_b_d[layer:layer + 1, qq * 1408:(qq + 1) * 1408])
                        A("sp", lambda e, j=j, qq=qq, src=src: e.dma_start(out=convp[:, j, qq * 11:(qq + 1) * 11], in_=src.rearrange("o (c p) -> p (o c)", p=P)),
                          w=["convp"], dma="convp")
            A("pool", lambda e: e.memset(halo[:], 0.0), w=[("halo", fc) for fc in range(88)])
            actT = wv(F_ACT, [P, FQ, 512], BF16)
            for g in range(NG):
                norm_mod(g, CV_A + 2 + layer, layer, 3)
                for q in range(NQ):
                    for j in range(FQ):
                        f = q * FQ + j
                        accs = []
                        for half in range(2):
                            fc = f + half * FC
                            b, ps = proj_fm(ffn_w_up_d[layer], fc * P)
                            acc = wv(F_ACC + ((f % 2) * 2 + half) * 2048, [P, 512], F32)
                            akey = ("acc", f % 2, half)
                            A("act", lambda e, acc=acc, ps=ps, fc=fc: e.activation(out=acc, in_=ps, func=AF.Identity, scale=convp[:, 2, fc:fc + 1], bias=convp[:, 3, fc:fc + 1]),
                              r=[("ps", b), "convp"], w=[akey])
                            A("dve", lambda e, acc=acc, ps=ps, fc=fc: e.scalar_tensor_tensor(out=acc[:, 1:512], in0=ps[:, 0:511], scalar=convp[:, 1, fc:fc + 1], in1=acc[:, 1:512],
                                                                                       op0=ALU.mult, op1=ALU.add), r=[("ps", b), "convp", akey], w=[akey])
                            A("dve", lambda e, acc=acc, ps=ps, fc=fc: e.scalar_tensor_tensor(out=acc[:, 2:512], in0=ps[:, 0:510], scalar=convp[:, 0, fc:fc + 1], in1=acc[:, 2:512],
                                                                                       op0=ALU.mult, op1=ALU.add), r=[("ps", b), "convp", akey], w=[akey])
                            A("dve", lambda e, acc=acc, fc=fc: e.scalar_tensor_tensor(out=acc[:, 0:1], in0=halo[:, fc, 1:2], scalar=convp[:, 1, fc:fc + 1], in1=acc[:, 0:1],
                                                                                op0=ALU.mult, op1=ALU.add), r=[("halo", fc), "convp", akey], w=[akey])
                            A("dve", lambda e, acc=acc, fc=fc: e.scalar_tensor_tensor(out=acc[:, 0:2], in0=halo[:, fc, 0:2], scalar=convp[:, 0, fc:fc + 1], in1=acc[:, 0:2],
                                                                                op0=ALU.mult, op1=ALU.add), r=[("halo", fc), "convp", akey], w=[akey])
                            A("dve", lambda e, ps=ps, fc=fc: e.tensor_copy(out=halo[:, fc, :], in_=ps[:, 510:512]), r=[("ps", b), ("halo", fc)], w=[("halo", fc)])
                            accs.append((acc, akey))
                        (ag, kg), (av, kv) = accs
                        A("act", lambda e, ag=ag: e.activation(out=ag, in_=ag, func=AF.Silu), r=[kg], w=[kg])
                        A("dve", lambda e, ag=ag, av=av, j=j: e.tensor_tensor(out=actT[:, j, :], in0=ag, in1=av, op=ALU.mult), r=[kg, kv], w=["actT"])
                    for dc in range(DC):
                        b = 4 + (pcnt[0] % 2); pcnt[0] += 1
                        slot, view = load_w_rows(ffn_w_down_d[layer, q * FQ * P:(q + 1) * FQ * P, dc * P:(dc + 1) * P], FQ)
                        for j in range(FQ):
                            A("pe", lambda e, j=j, b=b, view=view: e.matmul(psum[b][:, :], lhsT=view[:, j, :], rhs=actT[:, j, :], start=(j == 0), stop=(j == FQ - 1)),
                              r=[("wr", slot), "actT"], w=[("ps", b)])
                        A("dve", lambda e, b=b, dc=dc, g=g: e.scalar_tensor_tensor(out=xT[:, dc, g * GT:(g + 1) * GT], in0=psum[b][:, :], scalar=cv(CV_G + 2 + layer, dc),
                                                                                   in1=xT[:, dc, g * GT:(g + 1) * GT], op0=ALU.mult, op1=ALU.add),
                          r=[("ps", b), ("cvec", CV_G + 2 + layer), ("xT", g)], w=[("xT", g)])

        def fox_layer():
            A = pg.add
            layer = 1
            X_SQ = W_FREE
            X_RS = X_SQ + 2048
            X_ST = X_RS + 4096
            X_Z = 30720
            bias_all = wv(20480, [P, 136, 16], F32)
            with nc.allow_non_contiguous_dma(reason="tiny fl weights"):
                A("pool", lambda e: e.dma_start(out=wsmall[:], in_=fox_w_in_d[:, 8192:8208].rearrange("(a p) n -> p a n", p=P)),
                  w=["wsmall"], dma="wsm")
            for f_ in convert_fox_weights():
                f_()
            A("pool", lambda e: e.memset(rsum[:], 0.0), w=["rsum"])
            A("pool", lambda e: e.memset(uf, 1.0), w=["uf"])
            A("pool", lambda e: e.affine_select(out=uf, in_=uf, pattern=[[1, P]], compare_op=ALU.is_ge,
                                                 fill=0.0, base=0, channel_multiplier=-1), r=["uf"], w=["uf"])
            A("pool", lambda e: e.memset(negm, 0.0), w=["negm"])
            A("pool", lambda e: e.affine_select(out=negm, in_=negm, pattern=[[1, P]], compare_op=ALU.is_ge,
                                                 fill=-30000.0, base=0, channel_multiplier=-1), r=["negm"], w=["negm"])
            for g in range(NG):
                norm_mod(g, CV_A + layer, layer, 0)
                for tl in range(4):
                    t = g * 4 + tl
                    pz = psum[0]
                    for kc in range(DC):
                        A("pe", lambda e, kc=kc, tl=tl: e.matmul(pz[:, 0:16], lhsT=hT[:, kc, tl * P:(tl + 1) * P], rhs=wsmall[:, kc, :], start=(kc == 0), stop=(kc == 15)),
                          r=[("hT", kc), "wsmall"], w=[("ps", 0)])
                    zf = wv(X_Z, [P, 16], F32)
                    A("dve", lambda e, zf=zf: e.tensor_tensor(out=zf, in0=pz[:, 0:16], in1=bfb[:], op=ALU.add), r=[("ps", 0), "bfb"], w=["zf"])
                    A("act", lambda e, zf=zf: e.activation(out=zf, in_=zf, func=AF.Exp, scale=-1.0), r=["zf"], w=["zf"])
                    A("act", lambda e, zf=zf, t=t: e.activation(out=nlf_all[:, t, :], in_=zf, func=AF.Ln, bias=1.0), r=["zf"], w=[("nlf", t)])
                    pc = psum[1]
                    A("pe", lambda e: e.matmul(pc[:, 0:16], lhsT=ones_f[:], rhs=rsum[:], start=True, stop=True), r=["ones_f", "rsum"], w=[("ps", 1)])
                    A("pe", lambda e, t=t: e.matmul(pc[:, 16:32], lhsT=ones_f[:], rhs=rsum[:], start=True, stop=False), r=["ones_f", "rsum"], w=[("ps", 1)])
                    A("pe", lambda e, t=t: e.matmul(pc[:, 16:32], lhsT=uf, rhs=nlf_all[:, t, :], start=False, stop=True), r=["uf", ("nlf", t)], w=[("ps", 1)])
                    A("dve", lambda e, t=t: e.tensor_copy(out=ncum[:, t, :], in_=pc[:, 16:32]), r=[("ps", 1)], w=[("ncum", t)])
                    o0 = t * (t + 1) // 2
                    ref = wv(X_Z + 64, [P, 16], F32)
                    A("dve", lambda e, ref=ref: e.tensor_copy(out=ref, in_=pc[:, 0:16]), r=[("ps", 1)], w=["fref"])
                    A("dve", lambda e, t=t, o0=o0, ref=ref: e.scalar_tensor_tensor(
                        out=bias_all[:, o0:o0 + t + 1, :], in0=ncum[:, 0:t + 1, :], scalar=-BFOX,
                        in1=ref.unsqueeze(1).to_broadcast([P, t + 1, 16]), op0=ALU.add, op1=ALU.subtract),
                      r=[("ncum", j) for j in range(t + 1)] + ["fref"], w=[("fbias", t)])
                    A("dve", lambda e, t=t: e.tensor_tensor(out=rsum[:], in0=rsum[:], in1=nlf_all[:, t, :], op=ALU.add), r=["rsum", ("nlf", t)], w=["rsum"])
                pending = []
                for which in range(2):
                    dst = qd_s if which == 0 else ki_s
                    dkey = "qd" if which == 0 else "ki"
                    gsc = gcol[:, 2:3] if which == 0 else gcol[:, 1:2]
                    for h in range(16):
                        if h % 2 == 0:
                            pkeys, pview = load_w_pair(which * 8 + h // 2)
                        b, ps = proj_mm(lambda kc, pview=pview, sub=h % 2: pview[:, kc, sub * P:(sub + 1) * P], pkeys)
                        sq = wv(X_SQ + (h % 2) * 1024, [P, 512], BF16)
                        A("act", lambda e, sq=sq, ps=ps: e.activation(out=sq, in_=ps, func=AF.Square), r=[("ps", b)], w=[("fsq", h % 2)])

                        def post(b=b, ps=ps, sq=sq, h=h, gsc=gsc, dst=dst, dkey=dkey):
                            pm = psum[6 + (h % 2)]
                            A("pe", lambda e: e.matmul(pm[:, :], lhsT=ones_b[:], rhs=sq, start=True, stop=True), r=[("fsq", h % 2), "ones_b"], w=[("ps", 6 + (h % 2))])
                            rs_ = wv(X_RS + (h % 2) * 2048, [P, 512], F32)
                            A("act", lambda e: e.activation(out=rs_, in_=pm[:, :], func=AF.Ln, bias=float(EPS), scale=1.0 / 128.0), r=[("ps", 6 + (h % 2))], w=[("frs", h % 2)])
                            A("act", lambda e: e.activation(out=rs_, in_=rs_, func=AF.Exp, scale=-0.5), r=[("frs", h % 2)], w=[("frs", h % 2)])

                            def fill(st, key):
                                A("dve", lambda e: e.scalar_tensor_tensor(out=st, in0=ps, scalar=gsc, in1=rs_, op0=ALU.mult, op1=ALU.mult),
                                  r=[("ps", b), ("frs", h % 2), "gcol", "gcol2"], w=[key])
                            stage_store("fst", 6, X_ST, [P, 512], BF16, fill, dst[h, :, g * GT:(g + 1) * GT], [(dkey, g, h)])
                        pending.append(post)
                        if len(pending) > 1:
                            pending.pop(0)()
                while pending:
                    pending.pop(0)()
                for h in range(16):
                    if h % 2 == 0:
                        pkeys, pview = load_w_pair(24 + h // 2)
                    b, ps = proj_mm(lambda kc, pview=pview, sub=h % 2: pview[:, kc, sub * P:(sub + 1) * P], pkeys)

                    def fill(st, key, b=b, ps=ps):
                        A("act", lambda e: e.activation(out=st, in_=ps, func=AF.Sigmoid), r=[("ps", b)], w=[key])
                    stage_store("fst", 6, X_ST, [P, 512], BF16, fill, rs_s[h, :, g * GT:(g + 1) * GT], [("rs", g, h)])
                for cb in range(4):
                    proj_tm(fox_w_in_d, 2 * D + cb * 512, g, vv_s, cb * 512, X_ST, "fst", 6, pre=cb)
            if "foxp" in stages:
                return
            bar()
            fox_core(bias_all)
            bar()
            for g in range(NG):
                out_proj(g, fox_w_out_d, CV_G + layer, 16)

        def fox_core(bias_all):
            A = pg.add
            C_PT = W_FREE
            C_RL = C_PT + 1024
            C_T1 = C_RL + 1024
            C_Y = C_T1 + 1024
            bufs = []
            for i, base in enumerate((hT, wring)):
                flat = base[:, :, :].rearrange("p a b -> p (a b)")
                kn = flat[:, 0:2048]
                vv = flat[:, 2048:4096].rearrange("p (a b) -> p a b", a=16)
                qn = flat[:, 4096:4608]
                sg = flat[:, 4608:5120]
                bufs.append((kn, vv, qn, sg))
            ptc = 0
            iters = [(g_, h_) for g_ in range(NG) for h_ in range(16)]

            def emit_loads(k):
                g, h = iters[k]
                nk = (g + 1) * 4
                bi = k % 2
                kn, vv, qn, sg = bufs[bi]
                bk = ("fx", bi)
                wkeys = [bk] + ([("hT", d_) for d_ in range(DC)] if bi == 0 else [("wr", s_) for s_ in range(4)])
                A("sp", lambda e: e.dma_start(out=kn[:, 0:nk * P], in_=ki_s[h, :, 0:nk * P]),
                  r=[("ki", j, h) for j in range(g + 1)], w=wkeys, dma=bk)
                A("sp", lambda e: e.dma_start(out=vv[:, 0:nk, :], in_=vv_s[0:nk * P, h * P:(h + 1) * P].rearrange("(a p) n -> p a n", p=P)),
                  r=[("vv", j, h // 4, tl_) for j in range(g + 1) for tl_ in range(4)], w=wkeys, dma=bk)
                A("sp", lambda e: e.dma_start(out=qn, in_=qd_s[h, :, g * GT:(g + 1) * GT]), r=[("qd", g, h)], w=wkeys, dma=bk)
                A("sp", lambda e: e.dma_start(out=sg, in_=rs_s[h, :, g * GT:(g + 1) * GT]), r=[("rs", g, h)], w=wkeys, dma=bk)

            emit_loads(0)
            for it_k in range(len(iters)):
                g, h = iters[it_k]
                if it_k + 1 < len(iters):
                    emit_loads(it_k + 1)
                if True:
                    bi = it_k % 2
                    kn, vv, qn, sg = bufs[bi]
                    bk = ("fx", bi)
                    rk = [bk] + ([("hT", d_) for d_ in range(DC)] if bi == 0 else [("wr", s_) for s_ in range(4)])
                    yi = stcnt.get("fy", 0); stcnt["fy"] = yi + 1
                    yst = wv(C_Y + (yi % 2) * 1024, [P, 512], BF16)
                    ykey = ("fy", yi % 2)
                    pend = []
                    for il in range(4):
                        i = g * 4 + il
                        ob = i % 2
                        po = psum[4 + 2 * ob][:, 0:P]
                        pl = psum[5 + 2 * ob][:, 0:P]
                        okey = ("ps", 4 + 2 * ob); lkey = ("ps", 5 + 2 * ob)

                        def pv(j, pt, ptk, i=i, il=il, ob=ob, po=po, pl=pl, okey=okey, lkey=lkey, vv=vv, sg=sg, yst=yst, ykey=ykey, rk=rk):
                            A("pe", lambda e: e.matmul(po, lhsT=vv[:, j, :], rhs=pt, start=(j == 0), stop=(j == i)), r=rk + [ptk], w=[okey])
                            A("pe", lambda e: e.matmul(pl, lhsT=ones_b[:], rhs=pt, start=(j == 0), stop=(j == i)), r=["ones_b", ptk], w=[lkey])
                            if j == i:
                                rl = wv(C_RL + ob * 512, [P, P], F32)
                                A("dve", lambda e: e.reciprocal(out=rl, in_=pl), r=[lkey], w=[("frl", ob)])
                                t1 = wv(C_T1 + ob * 512, [P, P], F32)
                                A("dve", lambda e: e.tensor_tensor(out=t1, in0=po, in1=rl, op=ALU.mult), r=[okey, ("frl", ob)], w=[("ft1", ob)])
                                A("dve", lambda e: e.tensor_tensor(out=yst[:, il * P:(il + 1) * P], in0=t1, in1=sg[:, il * P:(il + 1) * P], op=ALU.mult),
                                  r=[("ft1", ob)] + rk, w=[ykey])
                        for j in range(i + 1):
                            sl = ptc % 4; ptc += 1
                            pst = psum[sl][:, 0:P]
                            skey = ("ps", sl)
                            A("pe", lambda e, pst=pst, j=j, il=il, kn=kn, qn=qn, i=i: e.matmul(pst, lhsT=kn[:, j * P:(j + 1) * P], rhs=qn[:, il * P:(il + 1) * P],
                                                                                         start=True, stop=(j != i)), r=rk, w=[skey])
                            if j == i:
                                A("pe", lambda e, pst=pst: e.matmul(pst, lhsT=ident_b[:], rhs=negm, start=False, stop=True), r=["ident_b", "negm"], w=[skey])
                            pt = wv(C_PT + (sl % 4) * 256, [P, P], BF16)
                            ptk = ("fpt", sl % 4)
                            o0 = i * (i + 1) // 2 + j
                            A("act", lambda e, pt=pt, pst=pst, o0=o0, h=h: e.activation(out=pt, in_=pst, func=AF.Exp, bias=bias_all[:, o0, h:h + 1]),
                              r=[skey, ("fbias", i)], w=[ptk])
                            pend.append((pv, j, pt, ptk))
                            if len(pend) > 3:
                                f_, j_, pt_, ptk_ = pend.pop(0)
                                f_(j_, pt_, ptk_)
                    while pend:
                        f_, j_, pt_, ptk_ = pend.pop(0)
                        f_(j_, pt_, ptk_)
                    A("sp", lambda e, yst=yst, h=h, g=g: e.dma_start(out=yy_s[h, :, g * GT:(g + 1) * GT], in_=yst), r=[ykey], w=[("yy", g, h)], dma=ykey)

        def final_out():
            A = pg.add
            O_T = W_FREE
            O_ST = O_T + 1024
            for t in range(NT):
                rstd = rstd_group(t // 4, P, t * P)
                ost = wv(O_ST + (t % 2) * 8192, [P, D], F32)
                okey = ("ost", t % 2)
                for q in range(4):
                    bank = psum[q % 4]
                    for j in range(4):
                        dc = q * 4 + j
                        tmp = wv(O_T + (dc % 2) * 512, [P, P], F32)
                        A("dve", lambda e, tmp=tmp, dc=dc, t=t, rstd=rstd: e.scalar_tensor_tensor(out=tmp, in0=xT[:, dc, t * P:(t + 1) * P], scalar=cv(CV_NF, dc), in1=rstd,
                                                                                                op0=ALU.mult, op1=ALU.mult),
                          r=[("xT", t // 4), "rstd", ("cvec", CV_NF)], w=[("otmp", dc % 2)])
                        A("pe", lambda e, tmp=tmp, bank=bank, j=j: e.transpose(bank[:, j * P:(j + 1) * P], tmp, ident_f[:]), r=[("otmp", dc % 2), "ident_f"], w=[("ps", q % 4)])
                    if q % 2 == 0:
                        A("act", lambda e, ost=ost, bank=bank, q=q: e.activation(out=ost[:, q * 512:(q + 1) * 512], in_=bank[:, :], func=AF.Copy), r=[("ps", q % 4)], w=[okey])
                    else:
                        A("dve", lambda e, ost=ost, bank=bank, q=q: e.tensor_copy(out=ost[:, q * 512:(q + 1) * 512], in_=bank[:, :]), r=[("ps", q % 4)], w=[okey])
                A("sp", lambda e, ost=ost, t=t: e.dma_start(out=out_d[t * P:(t + 1) * P, :], in_=ost), r=[okey], w=[("out", t)], dma=okey)

        setup()
        load_x()
        bar()
        if "gla" in stages:
            mod_layer(0)
            bar()
            gla_layer()
            bar()
        if "ffn0" in stages:
            if "gla" not in stages:
                mod_layer(0)
                bar()
            ffn_layer(0)
            bar()
        if "fox" in stages or "foxp" in stages:
            mod_layer(1)
            bar()
            fox_layer()
            bar()
        if "ffn1" in stages:
            if "fox" not in stages:
                mod_layer(1)
                bar()
            ffn_layer(1)
            bar()
        final_out()

        def final_waits():
            res = []
            for k in (("ost", 0), ("ost", 1)):
                ops = [o for o in pg.ops if o.dma == k]
                if ops:
                    res.append((ops[-1].sem, ops[-1].val))
            return res

        block = es.enter_context(nc.Block())
        with nc.allow_non_contiguous_dma(reason="small strided parameter / per-head slices"):
            counts = pg.emit(nc, es, block, final_waits)
        nc._mk_counts = counts
        nc._mk_nsem = pg.nsem
    return nc


_INPUT_NAMES = ["x", "c", "w_mod", "b_mod", "norm_mix", "norm_ffn", "gla_w_in", "gla_w_gate", "gla_b_gate", "gla_norm",
                "gla_w_out", "fox_w_in", "fox_b_f", "fox_q_norm", "fox_k_norm", "fox_w_out", "ffn_w_up", "ffn_conv_w",
                "ffn_conv_b", "ffn_w_down", "norm_final"]


def make_in_maps(inputs, n):
    f = lambda a: np.ascontiguousarray(np.asarray(a, dtype=np.float32))
    shared = {
        "w_mod": f(inputs["w_mod"]), "b_mod": f(inputs["b_mod"]), "norm_mix": f(inputs["norm_mix"]), "norm_ffn": f(inputs["norm_ffn"]),
        "gla_w_in": f(inputs["gla_w_in"][0]), "gla_w_gate": f(inputs["gla_w_gate"][0]), "gla_b_gate": f(inputs["gla_b_gate"]),
        "gla_norm": f(inputs["gla_norm"]), "gla_w_out": f(inputs["gla_w_out"][0]),
        "fox_w_in": f(inputs["fox_w_in"][0]), "fox_b_f": f(inputs["fox_b_f"]), "fox_q_norm": f(inputs["fox_q_norm"]),
        "fox_k_norm": f(inputs["fox_k_norm"]), "fox_w_out": f(inputs["fox_w_out"][0]),
        "ffn_w_up": f(inputs["ffn_w_up"]), "ffn_conv_w": f(inputs["ffn_conv_w"]), "ffn_conv_b": f(inputs["ffn_conv_b"]),
        "ffn_w_down": f(inputs["ffn_w_down"]), "norm_final": f(inputs["norm_final"]).reshape(1, D),
    }
    x = f(inputs["x"]); c = f(inputs["c"])
    maps = []
    for b in range(n):
        m = dict(shared)
        m["x"] = x[b]
        m["c"] = c[b:b + 1]
        maps.append(m)
    return maps


def kernel(**inputs):
    n = 8
    nc = build_nc()
    in_maps = make_in_maps(inputs, n)
    res = run_bass_kernel_spmd(nc, in_maps, core_ids=list(range(n)))
    out = np.stack([np.asarray(r["out"], dtype=np.float32) for r in res.results], axis=0)
    return out
```
